# Optimizing a Trainium2 kernel written in Bass

```python
import math
import jax, jax.numpy as jnp
from jax import lax
import numpy as np

D_MODEL = 1024
BATCH = 4
SEQ = 8192
DEPTH = 4

CHUNK = 64
EPS = 1e-6
NEG_INF = -1e30
N_BRANCH = 3
BR_WIDTH = 512

SGU_BLOCK = 128
SGU_GROUPS = 8
SGU_GROUP_DIM = BR_WIDTH // SGU_GROUPS

MLA_HEADS = 8
MLA_NOPE = 64
MLA_ROPE = 32
MLA_V = 64
MLA_QK = MLA_NOPE + MLA_ROPE
MLA_Q_RANK = 256
MLA_KV_RANK = 128
ROPE_BASE = 10000.0
Q_BLOCK = 128

CA_HEADS = 8
CA_HEAD_DIM = BR_WIDTH // CA_HEADS
LEFT_CHUNKS = 8
BAND = (LEFT_CHUNKS + 1) * CHUNK
REL_CLIP = 128

IN_WIDTHS = (BR_WIDTH, BR_WIDTH, BR_WIDTH,
             MLA_Q_RANK, MLA_KV_RANK, MLA_ROPE, BR_WIDTH,
             BR_WIDTH, BR_WIDTH, BR_WIDTH, BR_WIDTH,
             N_BRANCH * D_MODEL)
D_IN = sum(IN_WIDTHS)

kernel_name = "hybrid_sgu_mla_chunkattn_streaming"


def rmsnorm(x, g):
    xf = x.astype(jnp.float32)
    y = xf * lax.rsqrt(jnp.mean(xf * xf, axis=-1, keepdims=True) + EPS)
    return (y * g.astype(jnp.float32)).astype(x.dtype)


def layernorm(x, g, b):
    xf = x.astype(jnp.float32)
    mu = jnp.mean(xf, axis=-1, keepdims=True)
    xc = xf - mu
    y = xc * lax.rsqrt(jnp.mean(xc * xc, axis=-1, keepdims=True) + EPS)
    return (y * g.astype(jnp.float32) + b.astype(jnp.float32)).astype(x.dtype)


def apply_rope(x, pos):
    half = x.shape[-1] // 2
    inv = ROPE_BASE ** (-jnp.arange(half, dtype=jnp.float32) / half)
    ang = pos.astype(jnp.float32)[:, None] * inv[None, :]
    cos = jnp.cos(ang)[:, None, :]
    sin = jnp.sin(ang)[:, None, :]
    xf = x.astype(jnp.float32)
    x1, x2 = xf[..., :half], xf[..., half:]
    return jnp.concatenate([x1 * cos - x2 * sin, x1 * sin + x2 * cos], axis=-1).astype(x.dtype)


def sgu_mixer(u, v, ln_g, ln_b, w_s, b_s):
    B, S, _ = u.shape
    nb = S // SGU_BLOCK
    v = layernorm(v, ln_g, ln_b)
    vb = v.reshape(B, nb, SGU_BLOCK, SGU_GROUPS, SGU_GROUP_DIM)
    tri = jnp.tril(jnp.ones((SGU_BLOCK, SGU_BLOCK), dtype=bool))
    ws = jnp.where(tri[None], w_s, 0.0).astype(v.dtype)
    mixed = jnp.einsum('gts,bnsgc->bntgc', ws, vb) + b_s.T.astype(v.dtype)[None, None, :, :, None]
    return u * mixed.reshape(B, S, BR_WIDTH)


def mla_mixer(q_down, kv_down, k_rope_in, q_norm_g, kv_norm_g, w_uq, w_ukv, pos):
    B, S, _ = q_down.shape
    cq = rmsnorm(q_down, q_norm_g)
    q = (cq @ w_uq).reshape(B, S, MLA_HEADS, MLA_QK)
    q = jnp.concatenate([q[..., :MLA_NOPE], apply_rope(q[..., MLA_NOPE:], pos)], axis=-1)
    ckv = rmsnorm(kv_down, kv_norm_g)
    kv = (ckv @ w_ukv).reshape(B, S, MLA_HEADS, MLA_NOPE + MLA_V)
    k_nope, v = kv[..., :MLA_NOPE], kv[..., MLA_NOPE:]
    k_r = apply_rope(k_rope_in[:, :, None, :], pos)
    k = jnp.concatenate([k_nope, jnp.broadcast_to(k_r, (B, S, MLA_HEADS, MLA_ROPE))], axis=-1)
    scale = MLA_QK ** -0.5
    nqb = S // Q_BLOCK
    qb = q.reshape(B, nqb, Q_BLOCK, MLA_HEADS, MLA_QK).transpose(1, 0, 2, 3, 4)
    key_chunk = jnp.arange(S) // CHUNK

    def block(args):
        qi, bi = args
        s = jnp.einsum('bqhd,bkhd->bhqk', qi, k).astype(jnp.float32) * scale
        q_chunk = (bi * Q_BLOCK + jnp.arange(Q_BLOCK)) // CHUNK
        mask = key_chunk[None, :] <= q_chunk[:, None]
        s = jnp.where(mask[None, None], s, NEG_INF)
        p = jax.nn.softmax(s, axis=-1).astype(v.dtype)
        return jnp.einsum('bhqk,bkhd->bqhd', p, v)

    o = lax.map(block, (qb, jnp.arange(nqb)))
    return o.transpose(1, 0, 2, 3, 4).reshape(B, S, MLA_HEADS * MLA_V)


def chunk_band_mixer(q, k, v, rel_table):
    B, S, _ = q.shape
    nc = S // CHUNK
    pad = LEFT_CHUNKS * CHUNK
    q = q.reshape(B, S, CA_HEADS, CA_HEAD_DIM)
    k = k.reshape(B, S, CA_HEADS, CA_HEAD_DIM)
    v = v.reshape(B, S, CA_HEADS, CA_HEAD_DIM)
    kp = jnp.pad(k, ((0, 0), (pad, 0), (0, 0), (0, 0)))
    vp = jnp.pad(v, ((0, 0), (pad, 0), (0, 0), (0, 0)))
    qc = q.reshape(B, nc, CHUNK, CA_HEADS, CA_HEAD_DIM).transpose(1, 0, 2, 3, 4)
    i = jnp.arange(CHUNK)
    j = jnp.arange(BAND)
    dist = i[:, None] + pad - j[None, :]
    idx = jnp.clip(dist, -REL_CLIP, REL_CLIP) + REL_CLIP
    bias = rel_table[:, idx].astype(jnp.float32)
    scale = CA_HEAD_DIM ** -0.5

    def chunk(args):
        qi, ci = args
        kb = lax.dynamic_slice_in_dim(kp, ci * CHUNK, BAND, axis=1)
        vb = lax.dynamic_slice_in_dim(vp, ci * CHUNK, BAND, axis=1)
        s = jnp.einsum('bqhd,bkhd->bhqk', qi, kb).astype(jnp.float32) * scale + bias[None]
        valid = j >= (LEFT_CHUNKS - ci) * CHUNK
        s = jnp.where(valid[None, None, None, :], s, NEG_INF)
        p = jax.nn.softmax(s, axis=-1).astype(vb.dtype)
        return jnp.einsum('bhqk,bkhd->bqhd', p, vb)

    o = lax.map(chunk, (qc, jnp.arange(nc)))
    return o.transpose(1, 0, 2, 3, 4).reshape(B, S, BR_WIDTH)


def setup_inputs(seed: int = 0) -> dict:
    key = jax.random.key(seed)
    ks = jax.random.split(key, 16)
    f32 = jnp.float32
    L, D = DEPTH, D_MODEL
    nrm = lambda k, shape, s: jax.random.normal(k, shape, f32) * s
    return {
        "x": jax.random.normal(ks[0], (BATCH, SEQ, D), f32),
        "w_in": nrm(ks[1], (L, D, D_IN), D ** -0.5),
        "pre_g": 1.0 + nrm(ks[2], (L, D), 0.1),
        "post_g": 1.0 + nrm(ks[3], (L, D), 0.1),
        "sgu_ln_g": 1.0 + nrm(ks[4], (L, BR_WIDTH), 0.1),
        "sgu_ln_b": nrm(ks[5], (L, BR_WIDTH), 0.02),
        "sgu_w": nrm(ks[6], (L, SGU_GROUPS, SGU_BLOCK, SGU_BLOCK), SGU_BLOCK ** -0.5),
        "sgu_b": 1.0 + nrm(ks[7], (L, SGU_GROUPS, SGU_BLOCK), 0.1),
        "mla_q_norm_g": 1.0 + nrm(ks[8], (L, MLA_Q_RANK), 0.1),
        "mla_kv_norm_g": 1.0 + nrm(ks[9], (L, MLA_KV_RANK), 0.1),
        "mla_w_uq": nrm(ks[10], (L, MLA_Q_RANK, MLA_HEADS * MLA_QK), MLA_Q_RANK ** -0.5),
        "mla_w_ukv": nrm(ks[11], (L, MLA_KV_RANK, MLA_HEADS * (MLA_NOPE + MLA_V)), MLA_KV_RANK ** -0.5),
        "ca_rel_bias": nrm(ks[12], (L, CA_HEADS, 2 * REL_CLIP + 1), 0.5),
        "w_branch": nrm(ks[13], (L, N_BRANCH, BR_WIDTH, D), BR_WIDTH ** -0.5),
        "gate_b": nrm(ks[14], (L, N_BRANCH, D), 0.1),
        "w_out": nrm(ks[15], (L, D, D), D ** -0.5),
    }


def reference(x, w_in, pre_g, post_g, sgu_ln_g, sgu_ln_b, sgu_w, sgu_b,
              mla_q_norm_g, mla_kv_norm_g, mla_w_uq, mla_w_ukv, ca_rel_bias,
              w_branch, gate_b, w_out):
    B, S, D = x.shape
    pos = jnp.arange(S)
    offsets = [0]
    for w in IN_WIDTHS:
        offsets.append(offsets[-1] + w)
    for l in range(DEPTH):
        xn = rmsnorm(x, pre_g[l])
        proj = xn @ w_in[l]
        (u_a, v_a, z_a, qd_b, kvd_b, kr_b, z_b,
         q_c, k_c, v_c, z_c, g_logits) = [proj[..., offsets[n]:offsets[n + 1]] for n in range(len(IN_WIDTHS))]
        y_a = sgu_mixer(u_a, v_a, sgu_ln_g[l], sgu_ln_b[l], sgu_w[l], sgu_b[l]) * jax.nn.silu(z_a)
        y_b = mla_mixer(qd_b, kvd_b, kr_b, mla_q_norm_g[l], mla_kv_norm_g[l],
                        mla_w_uq[l], mla_w_ukv[l], pos) * jax.nn.silu(z_b)
        y_c = chunk_band_mixer(q_c, k_c, v_c, ca_rel_bias[l]) * jax.nn.silu(z_c)
        ys = jnp.stack([y_a, y_b, y_c], axis=2)
        br = jnp.einsum('bsnc,ncd->bsnd', ys, w_branch[l])
        gates = jax.nn.sigmoid(g_logits.reshape(B, S, N_BRANCH, D) + gate_b[l])
        merged = jnp.sum(gates * br, axis=2)
        x = x + rmsnorm(merged @ w_out[l], post_g[l])
    return x
```

```python
import numpy as np
from contextlib import ExitStack
import concourse.bass as bass
import concourse.mybir as mybir
from concourse.bass_utils import run_bass_kernel_spmd

F32 = mybir.dt.float32
BF16 = mybir.dt.bfloat16
AF = mybir.ActivationFunctionType
ALU = mybir.AluOpType

D = 1024
DEPTH = 4
BATCH = 4
SEQ = 8192
EPS = 1e-6
NT = 512
W_A = 3008
W_D = 4608
RELW = 1152


class Buf:
    __slots__ = ("name", "last_w", "readers")

    def __init__(self, name=""):
        self.name = name
        self.last_w = None
        self.readers = []


class Op:
    __slots__ = ("eng", "fn", "deps", "dma", "sig", "cnt", "slot", "slotv")

    def __init__(self, eng, fn, dma):
        self.eng = eng
        self.fn = fn
        self.dma = dma
        self.deps = []
        self.sig = False
        self.cnt = 0
        self.slot = 0
        self.slotv = 0


ENGS = ("pe", "act", "dve", "pool", "sp")
STRICT_SAME_ENGINE = True
EPOCH = 30000
DMA_RING = 8


class Prog:
    def __init__(self, nc):
        self.nc = nc
        self.ops = []
        self.stack = ExitStack()
        self.last = {e: None for e in ENGS}
        self.dmas_since_bar = []

    def sb(self, stack, name, shape, dt):
        self.nsb = getattr(self, "nsb", 0) + 1
        return stack.enter_context(self.nc.sbuf_tensor(f"{name}_{self.nsb}", shape, dt))

    def op(self, eng, fn, reads=(), writes=(), dma=False):
        o = Op(eng, fn, dma)
        deps = {}
        for b in reads:
            if b.last_w is not None:
                deps[id(b.last_w)] = b.last_w
        for b in writes:
            if b.last_w is not None:
                deps[id(b.last_w)] = b.last_w
            for r in b.readers:
                deps[id(r)] = r
        for b in reads:
            b.readers.append(o)
        for b in writes:
            b.last_w = o
            b.readers = []
        for d in deps.values():
            if d is o:
                continue
            if (not d.dma) and d.eng == eng and (eng == "pe" or not STRICT_SAME_ENGINE):
                continue
            o.deps.append(d)
            d.sig = True
        self.ops.append(o)
        if dma:
            self.dmas_since_bar.append(o)
        else:
            self.last[eng] = o
        return o

    def barrier(self):
        deps = [o for o in self.last.values() if o is not None] + list(self.dmas_since_bar)
        for d in deps:
            d.sig = True
        for e in ENGS:
            o = Op(e, None, False)
            o.deps = list(deps)
            self.ops.append(o)
        self.dmas_since_bar = []

    def dma(self, eng, out, in_, r=(), w=()):
        return self.op(eng, lambda e: e.dma_start(out=out, in_=in_), r, w, dma=True)

    def mm(self, out, lhsT, rhs, start, stop, r, w):
        return self.op("pe", lambda e: e.matmul(out, lhsT, rhs, start=start, stop=stop), r, w)

    def act(self, out, in_, func, r, w, bias=None, scale=None):
        kw = {}
        if bias is not None:
            kw["bias"] = bias
        if scale is not None:
            kw["scale"] = scale
        return self.op("act", lambda e: e.activation(out=out, in_=in_, func=func, **kw), r, w)

    def copy(self, eng, out, in_, r, w):
        if eng == "act":
            return self.act(out, in_, AF.Copy, r, w)
        return self.op(eng, lambda e: e.tensor_copy(out=out, in_=in_), r, w)

    def tt(self, eng, out, in0, in1, op, r, w):
        return self.op(eng, lambda e: e.tensor_tensor(out=out, in0=in0, in1=in1, op=op), r, w)

    def ts(self, eng, out, in0, s1, s2, op0, op1, r, w):
        if op1 is None:
            return self.op(eng, lambda e: e.tensor_scalar(out=out, in0=in0, scalar1=s1, scalar2=None, op0=op0), r, w)
        return self.op(eng, lambda e: e.tensor_scalar(out=out, in0=in0, scalar1=s1, scalar2=s2, op0=op0, op1=op1), r, w)

    def stt(self, eng, out, in0, scalar, in1, op0, op1, r, w):
        return self.op(eng, lambda e: e.scalar_tensor_tensor(out=out, in0=in0, scalar=scalar, in1=in1, op0=op0, op1=op1), r, w)

    def recip(self, out, in_, r, w):
        return self.op("dve", lambda e: e.reciprocal(out=out, in_=in_), r, w)

    def bn_stats(self, out, in_, r, w):
        return self.op("dve", lambda e: e.bn_stats(out=out, in_=in_), r, w)

    def bn_aggr(self, out, in_, r, w):
        return self.op("dve", lambda e: e.bn_aggr(out=out, in_=in_), r, w)

    def memset(self, eng, ap, val, w):
        return self.op(eng, lambda e: e.memset(ap, val), (), w)

    def finish(self, final_ops):
        nc = self.nc
        for o in final_ops:
            o.sig = True
        cnt = {e: 0 for e in ENGS}
        dcnt = {e: 0 for e in ENGS}
        for o in self.ops:
            if o.dma:
                m = dcnt[o.eng]
                dcnt[o.eng] += 1
                o.slot = m % DMA_RING
                o.slotv = m // DMA_RING
            elif o.sig:
                cnt[o.eng] += 1
                o.cnt = cnt[o.eng]
        st = self.stack
        sems = {e: [st.enter_context(nc.semaphore(f"s_{e}{i}")) for i in range(cnt[e] // EPOCH + 1)] for e in ENGS}
        rings = {e: ([st.enter_context(nc.semaphore(f"d_{e}{i}")) for i in range(DMA_RING)] if dcnt[e] else [])
                 for e in ENGS}
        ops = self.ops

        def emit(engname, e):
            known = {}

            def wait(key, sem, val):
                if known.get(key, 0) >= val:
                    return
                known[key] = val
                e.wait_ge(sem, val)

            for o in ops:
                if o.eng != engname:
                    continue
                for d in o.deps:
                    if d.dma:
                        wait(("d", d.eng, d.slot), rings[d.eng][d.slot], 16 * (d.slotv + 1))
                    else:
                        ep = (d.cnt - 1) // EPOCH
                        wait(("c", d.eng, ep), sems[d.eng][ep], (d.cnt - 1) % EPOCH + 1)
                if o.fn is None:
                    continue
                if o.dma:
                    if o.slotv > 0:
                        wait(("d", o.eng, o.slot), rings[o.eng][o.slot], 16 * o.slotv)
                    o.fn(e).then_inc(rings[o.eng][o.slot], 16)
                else:
                    ins = o.fn(e)
                    if o.sig:
                        ins.then_inc(sems[o.eng][(o.cnt - 1) // EPOCH], 1)
            if engname == "sp":
                for d in final_ops:
                    wait(("d", d.eng, d.slot), rings[d.eng][d.slot], 16 * (d.slotv + 1))

        with nc.Block() as block:
            @block.tensor
            def _(e):
                emit("pe", e)

            @block.scalar
            def _(e):
                emit("act", e)

            @block.vector
            def _(e):
                emit("dve", e)

            @block.gpsimd
            def _(e):
                emit("pool", e)

            @block.sync
            def _(e):
                emit("sp", e)
        self.stack.close()


class Rot:
    def __init__(self, items):
        self.items = items
        self.i = 0

    def next(self):
        it = self.items[self.i % len(self.items)]
        self.i += 1
        return it


def build_program(S, L, dbg=False):
    nc = bass.Bass("TRN2", target_bir_lowering=False)
    NTILE = S // NT
    P = Prog(nc)

    def din(name, shape, dt=F32):
        return nc.dram_tensor(name, shape, dt, kind="ExternalInput").ap()

    def dscr(name, shape, dt):
        return nc.dram_tensor(name, shape, dt, kind=("ExternalOutput" if dbg else "Internal")).ap()

    xT = din("xT", [D, S])
    wA = din("wA", [L, D, W_A])
    wD = din("wD", [L, D, W_D])
    wuq = din("wuq", [L, 256, 1024])
    wukv = din("wukv", [L, 128, 1024])
    wbr = din("wbr", [L, 3, 512, D])
    wout = din("wout", [L, D, D])
    gcols = din("gcols", [L, 128, 43])
    lnrow = din("lnrow", [L, 2, 512])
    bsb = din("bsb", [L, 128, 4, 128])
    wsT = din("wsT", [L, 128, 8, 128])
    relM = din("relM", [L, 128, 8, RELW])
    cs4 = din("cs4", [64, S])
    tri = din("tri", [128, 128])
    yout = nc.dram_tensor("yout", [D, S], F32, kind="ExternalOutput").ap()

    XN = dscr("XN", [128, 8, S], BF16)
    XR = dscr("XR", [128, 8, S], F32)
    YA = dscr("YA", [128, 4, S], BF16)
    YB = dscr("YB", [128, 4, S], BF16)
    YC = dscr("YC", [128, 4, S], BF16)
    CQ = dscr("CQ", [128, 2, S], BF16)
    CKV = dscr("CKV", [128, S], BF16)
    KR = dscr("KR", [32, S], BF16)
    QC = dscr("QC", [128, 4, S], BF16)
    KC = dscr("KC", [128, 4, S], BF16)
    VC = dscr("VC", [S, 1024], BF16)
    MG = dscr("MG", [128, 8, S], BF16)
    DBG = dscr("DBG", [128, 2, S], F32) if dbg else None

    xT_v = xT.rearrange("(c p) s -> p c s", p=128)
    yout_v = yout.rearrange("(c p) s -> p c s", p=128)
    VC_v = VC.rearrange("(n p) f -> p n f", p=128)

    psum = [P.stack.enter_context(nc.psum_tensor(f"ps{i}", [128, 512], F32)) for i in range(8)]
    psB = [Buf(f"ps{i}") for i in range(8)]

    ones_bf = P.sb(P.stack, "ones_bf", [128, 128], BF16)
    B_ones = Buf("ones")
    P.memset("pool", ones_bf[:], 1.0, [B_ones])

    final_ops = []

    def load_chunks(dst, dstB, src_rows, ncols, stage, stageB, k, piece, cast_eng=None, pend=None):
        out = []
        off = 0
        while off < ncols:
            n = min(piece, ncols - off)

            def f(off=off, n=n):
                sl = k[0] % 2
                k[0] += 1
                if pend is None:
                    P.dma("sp", stage[sl][:, 0:n], src_rows[:, off:off + n], w=[stageB[sl]])
                    eng = cast_eng if cast_eng else ("dve" if k[0] % 2 else "act")
                    P.copy(eng, dst[:, off:off + n], stage[sl][:, 0:n], [stageB[sl]], [dstB])
                else:
                    P.dma("pool", stage[sl][:, 0:n], src_rows[:, off:off + n], w=[stageB[sl]])
                    while pend:
                        pend.pop(0)()
                    pend.append(lambda: P.copy(cast_eng, dst[:, off:off + n], stage[sl][:, 0:n], [stageB[sl]], [dstB]))
            out.append(f)
            off += n
        return out

    carryA = ExitStack()
    wA_sb = P.sb(carryA, "wA_sb", [128, 8, W_A], BF16)
    B_wA = Buf()
    with ExitStack() as tmpst:
        stage0 = [P.sb(tmpst, f"stage0{i}", [128, 1504], F32) for i in range(2)]
        stage0B = [Buf(), Buf()]
        k0 = [0]
        for c in range(8):
            for ch in load_chunks(wA_sb[:, c, :], B_wA, wA[0, c * 128:(c + 1) * 128, :], W_A, stage0, stage0B, k0, 1504):
                ch()
    P.barrier()

    for l in range(L):
        last = (l == L - 1)
        x_src = xT_v if l == 0 else XR
        x_dst = yout_v if last else XR

        with ExitStack() as st:
            gc = P.sb(st, "gcA", [128, 43], F32)
            B_gc = Buf()
            lng = P.sb(st, "lng", [128, 512], F32)
            lnb = P.sb(st, "lnb", [128, 512], F32)
            B_ln = Buf()
            bsb_sb = P.sb(st, "bsb_sb", [128, 4, 128], F32)
            B_bsb = Buf()
            ws_f = P.sb(st, "ws_f", [128, 8, 128], F32)
            ws_sb = P.sb(st, "ws_sb", [128, 8, 128], BF16)
            tri_sb = P.sb(st, "tri_sb", [128, 128], F32)
            B_ws = Buf()
            xt = [P.sb(st, f"xt{i}", [128, 8, NT], F32) for i in range(2)]
            B_xt = [Buf(), Buf()]
            sq = P.sb(st, "sqA", [128, 8, NT], BF16)
            B_sq = Buf()
            rstd = P.sb(st, "rstdA", [128, NT], F32)
            B_rstd = Buf()
            xn = [P.sb(st, f"xnA{i}", [128, 8, NT], BF16) for i in range(2)]
            B_xn = [Buf(), Buf()]
            uA2 = [P.sb(st, f"uA{i}", [128, 4, NT], BF16) for i in range(2)]
            B_uA2 = [Buf(), Buf()]
            qd_sb = P.sb(st, "qd_sb", [128, 3, NT], F32)
            qsq = P.sb(st, "qsq", [128, 3, NT], BF16)
            B_qd = Buf()
            B_qsq = Buf()
            rq = P.sb(st, "rq", [128, 2, NT], F32)
            B_rq = Buf()
            rq2 = P.sb(st, "rq2", [128, 2, NT], F32)
            B_rq2 = Buf()
            cq_sb = P.sb(st, "cq_sbA", [128, 3, NT], BF16)
            B_cq = Buf()
            cst = P.sb(st, "cstA", [64, NT], F32)
            B_cst = Buf()
            krt = P.sb(st, "krt", [64, NT], F32)
            kr_bf = P.sb(st, "kr_bf", [32, NT], BF16)
            krs = P.sb(st, "krs", [32, NT], F32)
            B_kr = Buf()
            qc_sb = P.sb(st, "qc_sbA", [128, 4, NT], BF16)
            kc_sb = P.sb(st, "kc_sbA", [128, 4, NT], BF16)
            B_qc = Buf()
            B_kc = Buf()
            vaug = [P.sb(st, f"vaugA{i}", [128, 4, 1024], BF16) for i in range(2)]
            B_vaug = [Buf(), Buf()]
            vn0 = [P.sb(st, f"vn0_{i}", [128, NT], F32) for i in range(2)]
            vn1 = [P.sb(st, f"vn1_{i}", [128, NT], F32) for i in range(2)]
            B_vn1 = [Buf(), Buf()]
            vn = [P.sb(st, f"vn{i}", [128, NT], BF16) for i in range(2)]
            B_vn0 = [Buf(), Buf()]
            B_vn = [Buf(), Buf()]
            stt6 = P.sb(st, "bnst", [128, 4, 6], F32)
            mv = P.sb(st, "bnmv", [128, 4, 4], F32)
            B_mv = Buf()
            ytmp2 = [P.sb(st, f"ytmp{i}", [128, 4, 128], F32) for i in range(2)]
            B_ytmp2 = [Buf(), Buf()]
            yA_sb = P.sb(st, "yA_sb", [128, 4, NT], BF16)
            B_yA = Buf()

            P.dma("sp", gc[:], gcols[l], w=[B_gc])
            P.dma("sp", lng[:], lnrow[l, 0:1, :].partition_broadcast(128), w=[B_ln])
            P.dma("sp", lnb[:], lnrow[l, 1:2, :].partition_broadcast(128), w=[B_ln])
            P.dma("sp", bsb_sb[:], bsb[l], w=[B_bsb])
            P.dma("sp", ws_f[:], wsT[l], w=[B_ws])
            P.dma("sp", tri_sb[:], tri[:, :], w=[B_ws])
            for g in range(8):
                P.tt("dve", ws_sb[:, g, :], ws_f[:, g, :], tri_sb[:], ALU.mult, [], [B_ws])
            for i in range(2):
                va = vaug[i][:].rearrange("p j (hp two e) -> p j hp two e", two=2, e=128)
                for j in range(4):
                    P.memset("pool", va[:, j, :, 0, 64:128], 1.0, [B_vaug[i]])
                    P.memset("pool", va[:, j, :, 1, 0:64], 1.0, [B_vaug[i]])

            rot = Rot([(psum[i], psB[i]) for i in range(1, 8)])
            vnb = [P.sb(st, f"vnb{i}", [128, NT], BF16) for i in range(2)]
            B_vnb = [Buf(), Buf()]
            vraw = [P.sb(st, f"vraw{i}", [128, NT], F32) for i in range(4)]
            B_vraw = [Buf() for _ in range(4)]
            vn4 = [vn[0], vn[1], vnb[0], vnb[1]]
            B_vn4 = [B_vn[0], B_vn[1], B_vnb[0], B_vnb[1]]

            def normA(t):
                t0 = t * NT
                X, BX = xt[t % 2], B_xt[t % 2]
                XNt, BXN = xn[t % 2], B_xn[t % 2]
                P.dma("sp", X[:], x_src[:, :, t0:t0 + NT], w=[BX])
                for c in range(8):
                    P.act(sq[:, c, :], X[:, c, :], AF.Square, [BX], [B_sq])
                for c in range(8):
                    P.mm(psum[0][:], ones_bf[:], sq[:, c, :], c == 0, c == 7, [B_ones, B_sq], [psB[0]])
                P.act(rstd[:], psum[0][:], AF.Ln, [], [psB[0], B_rstd], bias=EPS, scale=1.0 / D)
                P.act(rstd[:], rstd[:], AF.Exp, [], [B_rstd], scale=-0.5)
                for c in range(8):
                    P.stt("dve", XNt[:, c, :], X[:, c, :], gc[:, c:c + 1], rstd[:],
                          ALU.mult, ALU.mult, [BX, B_gc, B_rstd], [BXN])
                P.dma("pool", XN[:, :, t0:t0 + NT], XNt[:], r=[BXN])

            def projA(t):
                t0 = t * NT
                XNt, BXN = xn[t % 2], B_xn[t % 2]
                uA, B_uA = uA2[t % 2], B_uA2[t % 2]
                P.dma("sp", cst[:], cs4[0:64, t0:t0 + NT], w=[B_cst])

                def fm(off, M):
                    ps, pb = rot.next()
                    for c in range(8):
                        P.mm(ps[0:M, :], wA_sb[:, c, off:off + M], XNt[:, c, :], c == 0, c == 7, [B_wA, BXN], [pb])
                    return ps, pb

                vps = []
                for j in range(4):
                    ps, pb = rot.next()
                    for c in range(8):
                        P.mm(ps[:], XNt[:, c, j * 128:(j + 1) * 128], wA_sb[:, c, 512:1024], c == 0, c == 7,
                             [B_wA, BXN], [pb])
                    P.copy("act", vraw[j][:], ps[:], [], [pb, B_vraw[j]])
                for j in range(4):
                    ps, pb = fm(j * 128, 128)
                    P.copy("act", uA[:, j, :], ps[:], [], [pb, B_uA])
                for j in range(4):
                    P.bn_stats(stt6[:, j, :], vraw[j][:], [B_vraw[j]], [B_mv])
                    P.bn_aggr(mv[:, j, 0:2], stt6[:, j, :], [], [B_mv])
                for c3 in range(3):
                    ps, pb = fm(1024 + c3 * 128, 128)
                    P.copy("act", qd_sb[:, c3, :], ps[:], [], [pb, B_qd])
                    P.act(qsq[:, c3, :], ps[:], AF.Square, [], [pb, B_qsq])
                P.act(mv[:, :, 2], mv[:, :, 1], AF.Ln, [], [B_mv], bias=EPS, scale=1.0)
                ps, pb = fm(1408, 64)
                P.tt("dve", krt[32:64, :], ps[32:64, :], cst[32:64, :], ALU.mult, [B_cst], [pb, B_kr])
                P.tt("dve", krt[0:32, :], ps[0:32, :], cst[0:32, :], ALU.mult, [B_cst], [pb, B_kr])
                P.act(mv[:, :, 3], mv[:, :, 2], AF.Exp, [], [B_mv], scale=-0.5)
                P.stt("dve", mv[:, :, 2], mv[:, :, 0], -1.0, mv[:, :, 3], ALU.mult, ALU.mult, [], [B_mv])
                for cg in range(4):
                    ps, pb = fm(1472 + cg * 128, 128)
                    P.copy("act", qc_sb[:, cg, :], ps[:], [], [pb, B_qc])
                P.dma("pool", QC[:, :, t0:t0 + NT], qc_sb[:], r=[B_qc])
                P.copy("act", krs[0:32, :], krt[32:64, :], [], [B_kr])
                for j in range(4):
                    P.act(vn0[j % 2][:], vraw[j][:], AF.Identity, [B_mv, B_vraw[j]], [B_vn0[j % 2]], bias=mv[:, j, 2:3], scale=mv[:, j, 3:4])
                    P.tt("dve", vn1[j % 2][:], vn0[j % 2][:], lng[:], ALU.mult, [B_ln, B_vn0[j % 2]], [B_vn1[j % 2]])
                    P.tt("pool", vn4[j][:], vn1[j % 2][:], lnb[:], ALU.add, [B_ln, B_vn1[j % 2]], [B_vn4[j]])
                P.tt("pool", kr_bf[:], krt[0:32, :], krs[0:32, :], ALU.add, [], [B_kr])
                P.dma("pool", KR[:, t0:t0 + NT], kr_bf[:], r=[B_kr])
                for cg in range(4):
                    ps, pb = fm(1984 + cg * 128, 128)
                    P.copy("dve" if cg % 2 else "act", kc_sb[:, cg, :], ps[:], [], [pb, B_kc])
                P.dma("pool", KC[:, :, t0:t0 + NT], kc_sb[:], r=[B_kc])
                ps, pb = rot.next()
                P.mm(ps[:], ones_bf[:], qsq[:, 0, :], True, False, [B_ones, B_qsq], [pb])
                P.mm(ps[:], ones_bf[:], qsq[:, 1, :], False, True, [B_ones, B_qsq], [pb])
                P.act(rq[:, 0, :], ps[:], AF.Ln, [], [pb, B_rq], bias=EPS, scale=1.0 / 256)
                ps, pb = rot.next()
                P.mm(ps[:], ones_bf[:], qsq[:, 2, :], True, True, [B_ones, B_qsq], [pb])
                P.act(rq[:, 1, :], ps[:], AF.Ln, [], [pb, B_rq], bias=EPS, scale=1.0 / 128)
                VA_ = vaug[t % 2]
                BVA = B_vaug[t % 2]
                va = VA_[:].rearrange("p j (hp two e) -> p j hp two e", two=2, e=128)
                for j in range(4):
                    ps, pb = rot.next()
                    for c in range(8):
                        P.mm(ps[:], XNt[:, c, j * 128:(j + 1) * 128], wA_sb[:, c, 2496:3008], c == 0, c == 7,
                             [B_wA, BXN], [pb])
                    pv = ps[:].rearrange("p (hp two d) -> p hp two d", two=2, d=64)
                    P.copy("act", va[:, j, :, 0, 0:64], pv[:, :, 0, :], [], [pb, BVA])
                    P.copy("dve", va[:, j, :, 1, 64:128], pv[:, :, 1, :], [], [pb, BVA])
                P.dma("pool", VC_v[:, 4 * t:4 * t + 4, :], VA_[:], r=[BVA])
                P.act(rq2[:], rq[:], AF.Exp, [B_rq], [B_rq2], scale=-0.5)
                for c3 in range(3):
                    P.stt("dve", cq_sb[:, c3, :], qd_sb[:, c3, :], gc[:, 40 + c3:41 + c3],
                          rq2[:, 0 if c3 < 2 else 1, :], ALU.mult, ALU.mult, [B_qd, B_gc, B_rq2], [B_cq])
                P.dma("pool", CQ[:, :, t0:t0 + NT], cq_sb[:, 0:2, :], r=[B_cq])
                P.dma("pool", CKV[:, t0:t0 + NT], cq_sb[:, 2, :], r=[B_cq])
                for j in range(4):
                    ps2, pb2 = rot.next()
                    for g in range(8):
                        hp = (g % 2) * 64
                        cg = g // 2
                        P.mm(ps2[hp:hp + 64, cg * 128:(cg + 1) * 128], vn4[j][:, g * 64:(g + 1) * 64], ws_sb[:, g, :],
                             True, True, [B_vn4[j], B_ws], [pb2])
                    ytmp, B_ytmp = ytmp2[j % 2], B_ytmp2[j % 2]
                    P.tt("dve", ytmp[:], ps2[:].rearrange("p (c t) -> p c t", t=128), bsb_sb[:], ALU.add,
                         [B_bsb], [pb2, B_ytmp])
                    P.tt("pool", yA_sb[:, :, j * 128:(j + 1) * 128], ytmp[:], uA[:, :, j * 128:(j + 1) * 128],
                         ALU.mult, [B_ytmp, B_uA], [B_yA])
                P.dma("pool", YA[:, :, t0:t0 + NT], yA_sb[:], r=[B_yA])

            normA(0)
            for t in range(NTILE):
                if t + 1 < NTILE:
                    normA(t + 1)
                projA(t)
        carryA.close()
        P.barrier()

        with ExitStack() as st:
            ckv_sb = P.sb(st, "ckv_sb", [128, S], BF16)
            cqr = P.sb(st, "cqr", [128, 2, S], BF16)
            B_ckv = Buf()
            B_cqr = Buf()
            kT = [P.sb(st, f"kT{i}", [128, S], BF16) for i in range(2)]
            B_kT = [Buf(), Buf()]
            qT = [P.sb(st, f"qT{i}", [128, S], BF16) for i in range(2)]
            B_qT = [Buf(), Buf()]
            vT = [P.sb(st, f"vT{i}", [128, S // 128, 128], BF16) for i in range(2)]
            B_vT = [Buf(), Buf()]
            yb_sb = [P.sb(st, f"yb_sb{i}", [128, S], BF16) for i in range(2)]
            B_yb = [Buf(), Buf()]
            wuq_sb = P.sb(st, "wuq_sb", [128, 2, 1024], BF16)
            wukv_sb = P.sb(st, "wukv_sb", [128, 1024], BF16)
            B_wC = Buf()
            stage = [P.sb(st, f"stageC{i}", [128, 512], F32) for i in range(2)]
            stageB = [Buf(), Buf()]
            cst = [P.sb(st, f"cstC{i}", [64, NT], F32) for i in range(2)]
            B_cst = [Buf(), Buf()]
            t12 = P.sb(st, "t12", [64, NT], F32)
            t3 = P.sb(st, "t3", [32, NT], F32)
            B_t12 = Buf()
            NPC = 6
            p_sb = [P.sb(st, f"p_sbC{i}", [128, NT], BF16) for i in range(NPC)]
            B_p = [Buf() for _ in range(NPC)]
            pd = [P.sb(st, f"pdC{i}", [128, NT], BF16) for i in range(4)]
            B_pd = [Buf() for _ in range(4)]
            for i in range(4):
                P.memset("pool", pd[i][:], 0.0, [B_pd[i]])
            rec = P.sb(st, "recC", [128, NT], F32)
            B_rec = Buf()
            k = [0]
            for kc in range(2):
                for ch in load_chunks(wuq_sb[:, kc, :], B_wC, wuq[l, kc * 128:(kc + 1) * 128, :], 1024, stage, stageB, k, 512):
                    ch()
            for ch in load_chunks(wukv_sb[:, :], B_wC, wukv[l, :, :], 1024, stage, stageB, k, 512):
                ch()
            P.dma("sp", ckv_sb[:], CKV[:, :], w=[B_ckv])
            for kc in range(2):
                P.dma("sp", cqr[:, kc, :], CQ[:, kc, :], w=[B_cqr])
            for i in range(2):
                P.dma("sp", kT[i][0:32, :], KR[:, :], w=[B_kT[i]])
                P.memset("pool", kT[i][32:64, :], 0.0, [B_kT[i]])
                P.memset("pool", qT[i][32:64, :], 0.0, [B_qT[i]])
            P.memset("pool", vT[0][:, :, 64:128], 1.0, [B_vT[0]])
            P.memset("pool", vT[1][:, :, 0:64], 1.0, [B_vT[1]])
            grot = Rot([(psum[i], psB[i]) for i in range(0, 2)])
            srot = Rot([(psum[i], psB[i]) for i in range(2, 6)])
            orot = Rot([(psum[i], psB[i]) for i in range(6, 8)])
            scale = 96 ** -0.5
            cctr = [0]

            def gen_chunks(h):
                sl = h % 2
                K_, BK = kT[sl], B_kT[sl]
                Q_, BQ = qT[sl], B_qT[sl]
                V_, BV = vT[sl], B_vT[sl]
                voff = 0 if sl == 0 else 64
                out = []

                def kchunk(tt_):
                    ps, pb = grot.next()
                    P.mm(ps[64:128, :], wukv_sb[:, h * 128:h * 128 + 64], ckv_sb[:, tt_ * NT:(tt_ + 1) * NT], True, True,
                         [B_wC, B_ckv], [pb])
                    P.copy("dve", K_[64:128, tt_ * NT:(tt_ + 1) * NT], ps[64:128, :], [], [pb, BK])

                def vchunk(t8):
                    ps, pb = grot.next()
                    for jj in range(8):
                        kt = t8 * 8 + jj
                        P.mm(ps[:, jj * 64:(jj + 1) * 64], ckv_sb[:, kt * 128:(kt + 1) * 128],
                             wukv_sb[:, h * 128 + 64:h * 128 + 128], True, True, [B_wC, B_ckv], [pb])
                    P.copy("dve", V_[:, t8 * 8:(t8 + 1) * 8, voff:voff + 64],
                           ps[:].rearrange("p (j d) -> p j d", d=64), [], [pb, BV])

                def qchunk(tt_):
                    CS, BCS = cst[cctr[0] % 2], B_cst[cctr[0] % 2]
                    cctr[0] += 1
                    P.dma("sp", CS[:], cs4[0:64, tt_ * NT:(tt_ + 1) * NT], w=[BCS])
                    ps, pb = grot.next()
                    for kc in range(2):
                        P.mm(ps[:], wuq_sb[:, kc, h * 128:(h + 1) * 128], cqr[:, kc, tt_ * NT:(tt_ + 1) * NT], kc == 0, kc == 1,
                             [B_wC, B_cqr], [pb])
                    P.tt("dve", t12[0:64, :], ps[0:64, :], CS[0:64, :], ALU.mult, [BCS], [pb, B_t12])
                    P.copy("dve", Q_[64:128, tt_ * NT:(tt_ + 1) * NT], ps[64:128, :], [], [pb, BQ])
                    P.copy("act", t3[0:32, :], t12[32:64, :], [], [B_t12])
                    P.tt("pool", Q_[0:32, tt_ * NT:(tt_ + 1) * NT], t12[0:32, :], t3[0:32, :], ALU.add, [], [B_t12, BQ])

                for tt_ in range(NTILE):
                    out.append(lambda tt_=tt_: kchunk(tt_))
                for t8 in range(S // 1024):
                    out.append(lambda t8=t8: vchunk(t8))
                for tt_ in range(NTILE):
                    out.append(lambda tt_=tt_: qchunk(tt_))
                return out

            for ch in gen_chunks(0):
                ch()
            LA = 5
            it = 0
            for h in range(8):
                sl = h % 2
                c = h // 2
                K_, BK = kT[sl], B_kT[sl]
                Q_, BQ = qT[sl], B_qT[sl]
                V_, BV = vT[sl], B_vT[sl]
                Y_, BY = yb_sb[c % 2], B_yb[c % 2]
                nxt = gen_chunks(h + 1) if h + 1 < 8 else []
                items = [(g, j) for g in range(NTILE) for j in range(4 * g + 4)]
                every = max(1, len(items) // (len(nxt) + 1)) if nxt else 0
                pend = []
                obank = {}

                def emit_pv(item):
                    g, j, c0, PT, BP = item
                    nj = 4 * g + 4
                    if j == 0:
                        obank[g] = orot.next()
                    ops_, opb = obank[g]
                    P.mm(ops_[:, :], V_[:, j, :], PT[:, :], j == 0, j == nj - 1, [BV, BP], [opb])
                    if j == nj - 1:
                        if sl == 0:
                            P.recip(rec[64:128, :], ops_[64:128, :], [], [opb, B_rec])
                            P.tt("dve", Y_[0:64, g * NT:(g + 1) * NT], ops_[0:64, :], rec[64:128, :], ALU.mult, [B_rec], [opb, BY])
                        else:
                            P.recip(rec[0:64, :], ops_[0:64, :], [], [opb, B_rec])
                            P.tt("dve", Y_[64:128, g * NT:(g + 1) * NT], ops_[64:128, :], rec[0:64, :], ALU.mult, [B_rec], [opb, BY])
                        del obank[g]

                for idx, (g, j) in enumerate(items):
                    r_ = j - 4 * g
                    c0 = 128 * r_ if r_ > 0 else 0
                    sps, spb = srot.next()
                    P.mm(sps[:, c0:NT], K_[:, j * 128:(j + 1) * 128], Q_[:, g * NT + c0:(g + 1) * NT], True, True,
                         [BK, BQ], [spb])
                    if r_ >= 0:
                        PT, BP = pd[r_], B_pd[r_]
                    else:
                        PT, BP = p_sb[it % NPC], B_p[it % NPC]
                        it += 1
                    P.act(PT[:, c0:NT], sps[:, c0:NT], AF.Exp, [], [spb, BP], scale=scale)
                    if r_ >= 0:
                        P.memset("pool", PT[64:128, c0:c0 + 64], 0.0, [BP])
                    pend.append((g, j, c0, PT, BP))
                    if len(pend) > LA:
                        emit_pv(pend.pop(0))
                    if nxt and every and (idx % every == every - 1):
                        nxt.pop(0)()
                while pend:
                    emit_pv(pend.pop(0))
                while nxt:
                    nxt.pop(0)()
                if sl == 1:
                    P.dma("pool", YB[:, c, :], Y_[:], r=[BY])
        P.barrier()

        carryD = ExitStack()
        wD_sb = P.sb(carryD, "wD_sb", [128, 8, W_D], BF16)
        wbr_sb = P.sb(carryD, "wbr_sb", [128, 12, D], BF16)
        B_wD = Buf()
        stageD = [P.sb(carryD, f"stageD{i}", [128, 1152], F32) for i in range(2)]
        stageDB = [Buf(), Buf()]
        kD = [0]
        dchunks = []
        dpend = []
        for c in range(8):
            dchunks += load_chunks(wD_sb[:, c, :], B_wD, wD[l, c * 128:(c + 1) * 128, :], W_D, stageD, stageDB, kD, 1152, "pool", dpend)
        for i in range(3):
            for kc in range(4):
                dchunks += load_chunks(wbr_sb[:, i * 4 + kc, :], B_wD, wbr[l, i, kc * 128:(kc + 1) * 128, :], D, stageD, stageDB, kD,
                                       1024, "pool", dpend)
        with ExitStack() as st:
            rel_sb = P.sb(st, "rel_sb", [128, 8, RELW], F32)
            B_rel = Buf()
            P.dma("sp", rel_sb[:, 0:4, :], relM[l, :, 0:4, :], w=[B_rel])
            P.dma("sp", rel_sb[:, 4:8, :], relM[l, :, 4:8, :], w=[B_rel])
            qct = [P.sb(st, f"qct{i}", [128, 4, NT], BF16) for i in range(2)]
            B_qct = [Buf(), Buf()]
            kct = [P.sb(st, f"kct{i}", [128, 4, NT], BF16) for i in range(3)]
            B_kct = [Buf(), Buf(), Buf()]
            vt = [P.sb(st, f"vtB{i}", [128, 4, 1024], BF16) for i in range(3)]
            B_vt = [Buf(), Buf(), Buf()]
            NSC, NPB = 6, 8
            sc_sb = [P.sb(st, f"sc_sb{i}", [128, NT], BF16) for i in range(NSC)]
            cbias = P.sb(st, "cbias", [128, 8], F32)
            B_cb = Buf()
            P.copy("dve", cbias[:], rel_sb[:, :, RELW - 1], [B_rel], [B_cb])
            for h_ in range(8):
                P.act(rel_sb[:, h_, :], rel_sb[:, h_, :], AF.Exp, [B_cb], [B_rel])
            B_sc = [Buf() for _ in range(NSC)]
            p_sb = [P.sb(st, f"p_sbB{i}", [128, NT], BF16) for i in range(NPB)]
            B_p = [Buf() for _ in range(NPB)]
            for i in range(NPB):
                P.memset("pool", p_sb[i][:], 0.0, [B_p[i]])
            rec2 = [P.sb(st, f"recB{i}", [128, NT], F32) for i in range(2)]
            B_rec2 = [Buf(), Buf()]
            yc_sb = [P.sb(st, "yc_sb0", [128, 4, NT], BF16)] * 2
            B_yc = [Buf()] * 2
            srot = Rot([(psum[i], psB[i]) for i in range(0, 5)])
            orot = Rot([(psum[i], psB[i]) for i in range(5, 8)])
            scale = 64 ** -0.5
            LA = 6
            pend = []
            obank = {}

            def emit_pv(item):
                g, h, j, js, src, sub, c0, c1, PT, BP = item
                c = h // 2
                Y, BY = yc_sb[g % 2], B_yc[g % 2]
                if j == js[0]:
                    obank[(g, h)] = orot.next()
                ops_, opb = obank[(g, h)]
                P.mm(ops_[:, :], vt[src][:, sub, h * 128:(h + 1) * 128], PT[:, :], j == js[0], j == js[-1],
                     [B_vt[src], BP], [opb])
                if j == js[-1]:
                    RC, BRC = rec2[h % 2], B_rec2[h % 2]
                    if h % 2 == 0:
                        P.act(RC[64:128, :], ops_[64:128, :], AF.Ln, [], [opb, BRC])
                        P.act(RC[64:128, :], RC[64:128, :], AF.Exp, [], [BRC], scale=-1.0)
                        P.tt("dve", Y[0:64, c, :], ops_[0:64, :], RC[64:128, :], ALU.mult, [BRC], [opb, BY])
                    else:
                        P.act(RC[0:64, :], ops_[0:64, :], AF.Ln, [], [opb, BRC])
                        P.act(RC[0:64, :], RC[0:64, :], AF.Exp, [], [BRC], scale=-1.0)
                        P.tt("dve", Y[64:128, c, :], ops_[64:128, :], RC[0:64, :], ALU.mult, [BRC], [opb, BY])
                    del obank[(g, h)]
                    if h == 7:
                        P.dma("pool", YC[:, :, g * NT:(g + 1) * NT], Y[:], r=[BY])

            it = 0
            n_items = 8 * (8 * NTILE - 4)
            every_d = max(1, n_items // (len(dchunks) + 1))
            for g in range(NTILE):
                t0 = g * NT
                Q = qct[g % 2]
                BQ = B_qct[g % 2]
                P.dma("sp", Q[:], QC[:, :, t0:t0 + NT], w=[BQ])
                P.dma("sp", kct[g % 3][:], KC[:, :, t0:t0 + NT], w=[B_kct[g % 3]])
                P.dma("sp", vt[g % 3][:], VC_v[:, 4 * g:4 * g + 4, :], w=[B_vt[g % 3]])
                for h in range(8):
                    c = h // 2
                    hp = (h % 2) * 64
                    js = list(range(4, 8)) if g == 0 else list(range(8))
                    for j in js:
                        src = (g - 1) % 3 if j < 4 else g % 3
                        sub = j % 4
                        c0 = 64 * max(0, 2 * j - 8)
                        c1 = 64 * (min(7, 2 * j + 1) + 1)
                        cstart = 640 if j <= 2 else 512 - 128 * (j - 3)
                        sps, spb = srot.next()
                        P.mm(sps[:, c0:c1], kct[src][hp:hp + 64, c, sub * 128:(sub + 1) * 128], Q[hp:hp + 64, c, c0:c1],
                             True, True, [B_kct[src], BQ], [spb])
                        SC, BSC = sc_sb[it % NSC], B_sc[it % NSC]
                        PT, BP = p_sb[j], B_p[j]
                        it += 1
                        if j <= 2:
                            P.act(PT[:, c0:c1], sps[:, c0:c1], AF.Exp, [B_cb], [spb, BP], bias=cbias[:, h:h + 1], scale=scale)
                            meng = "pool"
                        else:
                            P.act(SC[:, c0:c1], sps[:, c0:c1], AF.Exp, [], [spb, BSC], scale=scale)
                            meng = "dve" if it % 3 else "pool"
                            P.tt(meng, PT[:, c0:c1], SC[:, c0:c1], rel_sb[:, h, cstart + c0:cstart + c1], ALU.mult,
                                 [B_rel, BSC], [BP])
                        if j <= 3:
                            P.memset(meng, PT[0:64, c1 - 64:c1], 0.0, [BP])
                        else:
                            P.memset(meng, PT[64:128, c0:c0 + 64], 0.0, [BP])
                        pend.append((g, h, j, js, src, sub, c0, c1, PT, BP))
                        while len(pend) > (3 if g == 0 else LA):
                            emit_pv(pend.pop(0))
                        if dchunks and it % every_d == 0:
                            dchunks.pop(0)()
            while pend:
                emit_pv(pend.pop(0))
            while dchunks:
                dchunks.pop(0)()
            while dpend:
                dpend.pop(0)()
        P.barrier()

        with ExitStack() as st:
            gc = P.sb(st, "gcD", [128, 43], F32)
            B_gc = Buf()
            xn = [P.sb(st, f"xnD{i}", [128, 8, NT], BF16) for i in range(2)]
            B_xn = [Buf(), Buf()]
            ys = [[P.sb(st, f"yD{i}_{b}", [128, 4, NT], BF16) for b in range(2)] for i in range(3)]
            B_ys = [[Buf(), Buf()] for _ in range(3)]
            yz2 = [[P.sb(st, f"yz{i}_{b}", [128, 4, NT], BF16) for i in range(3)] for b in range(2)]
            B_yz2 = [[Buf() for _ in range(3)] for _ in range(2)]
            sz = [P.sb(st, f"sz{i}", [128, NT], F32) for i in range(2)]
            B_sz = [Buf(), Buf()]
            gsb = [P.sb(st, f"gsb{i}", [128, NT], F32) for i in range(3)]
            B_gsb = [Buf() for _ in range(3)]
            acc = [P.sb(st, f"acc{i}", [128, NT], F32) for i in range(2)]
            B_acc = [Buf(), Buf()]
            tmp = [P.sb(st, f"tmpD{i}", [128, NT], F32) for i in range(2)]
            B_tmp = [Buf(), Buf()]
            mg = [P.sb(st, f"mg{i}", [128, 8, NT], BF16) for i in range(2)]
            B_mg = [Buf(), Buf()]
            P.dma("sp", gc[:], gcols[l], w=[B_gc])
            rot = Rot([(psum[i], psB[i]) for i in range(8)])
            ysrc = [YA, YB, YC]
            si = 0
            gi = 0
            ti = 0
            for t in range(NTILE):
                t0 = t * NT
                XNt, BXN = xn[t % 2], B_xn[t % 2]
                yz, B_yz = yz2[t % 2], B_yz2[t % 2]
                P.dma("sp", XNt[:], XN[:, :, t0:t0 + NT], w=[BXN])
                for i in range(3):
                    P.dma("sp", ys[i][t % 2][:], ysrc[i][:, :, t0:t0 + NT], w=[B_ys[i][t % 2]])
                for i in range(3):
                    for c in range(4):
                        ps, pb = rot.next()
                        for kc in range(8):
                            P.mm(ps[:], wD_sb[:, kc, i * 512 + c * 128:i * 512 + (c + 1) * 128], XNt[:, kc, :], kc == 0, kc == 7,
                                 [B_wD, BXN], [pb])
                        SZ, BSZ = sz[si % 2], B_sz[si % 2]
                        si += 1
                        P.act(SZ[:], ps[:], AF.Silu, [], [pb, BSZ])
                        P.tt("dve" if si % 2 else "pool", yz[i][:, c, :], ys[i][t % 2][:, c, :], SZ[:], ALU.mult,
                             [B_ys[i][t % 2], BSZ], [B_yz[i]])
                MGt, BMG = mg[t % 2], B_mg[t % 2]
                for m in range(8):
                    AC, BAC = acc[m % 2], B_acc[m % 2]
                    for i in range(3):
                        ps, pb = rot.next()
                        off = 1536 + i * 1024 + m * 128
                        for kc in range(8):
                            P.mm(ps[:], wD_sb[:, kc, off:off + 128], XNt[:, kc, :], kc == 0, kc == 7, [B_wD, BXN], [pb])
                        G, BG = gsb[gi % 3], B_gsb[gi % 3]
                        gi += 1
                        P.act(G[:], ps[:], AF.Sigmoid, [B_gc], [pb, BG], bias=gc[:, 16 + i * 8 + m:17 + i * 8 + m])
                        ps2, pb2 = rot.next()
                        for kc in range(4):
                            P.mm(ps2[:], wbr_sb[:, i * 4 + kc, m * 128:(m + 1) * 128], yz[i][:, kc, :], kc == 0, kc == 3,
                                 [B_wD, B_yz[i]], [pb2])
                        if i == 0:
                            P.tt("dve", AC[:], ps2[:], G[:], ALU.mult, [BG], [pb2, BAC])
                        else:
                            T_, BT = tmp[ti % 2], B_tmp[ti % 2]
                            ti += 1
                            P.tt("dve", T_[:], ps2[:], G[:], ALU.mult, [BG], [pb2, BT])
                            if i == 1:
                                P.tt("pool", AC[:], AC[:], T_[:], ALU.add, [BT], [BAC])
                            else:
                                P.tt("pool", MGt[:, m, :], AC[:], T_[:], ALU.add, [BT, BAC], [BMG])
                P.dma("pool", MG[:, :, t0:t0 + NT], MGt[:], r=[BMG])
        carryD.close()
        P.barrier()

        if not last:
            carryA = ExitStack()
            wA_sb = P.sb(carryA, "wA_sb", [128, 8, W_A], BF16)
            B_wA = Buf()
        with ExitStack() as st:
            wo_sb = P.sb(st, "wo_sb", [128, 8, D], BF16)
            B_wo = Buf()
            stage = [P.sb(st, f"stageE{i}", [128, 1024], F32) for i in range(2)]
            stageB = [Buf(), Buf()]
            gc = P.sb(st, "gcE", [128, 43], F32)
            B_gc = Buf()
            mgt = [P.sb(st, f"mgE{i}", [128, 8, NT], BF16) for i in range(2)]
            B_mgt = [Buf(), Buf()]
            xt = [P.sb(st, f"xtE{i}", [128, 8, NT], F32) for i in range(2)]
            B_xt = [Buf(), Buf()]
            o_sb = P.sb(st, "o_sb", [128, 8, NT], F32)
            B_o = Buf()
            sq = P.sb(st, "sqE", [128, 8, NT], BF16)
            B_sq = Buf()
            rstd = P.sb(st, "rstdE", [128, NT], F32)
            B_rstd = Buf()
            tmp = [P.sb(st, f"tmpE{i}", [128, NT], F32) for i in range(3)]
            B_tmp = [Buf(), Buf(), Buf()]
            xo = [P.sb(st, f"xoE{i}", [128, 8, NT], F32) for i in range(2)]
            B_xo = [Buf(), Buf()]
            k = [0]
            for c in range(8):
                for ch in load_chunks(wo_sb[:, c, :], B_wo, wout[l, c * 128:(c + 1) * 128, :], D, stage, stageB, k, 1024):
                    ch()
            achunks = []
            apend = []
            if not last:
                for c in range(8):
                    achunks += load_chunks(wA_sb[:, c, :], B_wA, wA[l + 1, c * 128:(c + 1) * 128, :], W_A, stage, stageB, k, 1024, "pool", apend)
            P.dma("sp", gc[:], gcols[l], w=[B_gc])
            rot = Rot([(psum[i], psB[i]) for i in range(1, 8)])
            o_sb2 = [o_sb, P.sb(st, "o_sbb", [128, 8, NT], F32)]
            B_o2 = [B_o, Buf()]
            sq2 = [sq, P.sb(st, "sqEb", [128, 8, NT], BF16)]
            B_sq2 = [B_sq, Buf()]

            def outproj(t):
                t0 = t * NT
                M_, BM = mgt[t % 2], B_mgt[t % 2]
                X, BX = xt[t % 2], B_xt[t % 2]
                O_, BO = o_sb2[t % 2], B_o2[t % 2]
                SQ, BSQ = sq2[t % 2], B_sq2[t % 2]
                P.dma("sp", M_[:], MG[:, :, t0:t0 + NT], w=[BM])
                P.dma("sp", X[:], x_src[:, :, t0:t0 + NT], w=[BX])
                for m in range(8):
                    ps, pb = rot.next()
                    for kc in range(8):
                        P.mm(ps[:], wo_sb[:, kc, m * 128:(m + 1) * 128], M_[:, kc, :], kc == 0, kc == 7, [B_wo, BM], [pb])
                    P.copy("dve", O_[:, m, :], ps[:], [], [pb, BO])
                    P.act(SQ[:, m, :], ps[:], AF.Square, [], [pb, BSQ])

            def post(t):
                t0 = t * NT
                X, BX = xt[t % 2], B_xt[t % 2]
                XO, BXO = xo[t % 2], B_xo[t % 2]
                O_, BO = o_sb2[t % 2], B_o2[t % 2]
                SQ, BSQ = sq2[t % 2], B_sq2[t % 2]
                for m in range(8):
                    P.mm(psum[0][:], ones_bf[:], SQ[:, m, :], m == 0, m == 7, [B_ones, BSQ], [psB[0]])
                P.act(rstd[:], psum[0][:], AF.Ln, [], [psB[0], B_rstd], bias=EPS, scale=1.0 / D)
                P.act(rstd[:], rstd[:], AF.Exp, [], [B_rstd], scale=-0.5)
                for m in range(8):
                    T_, BT = tmp[m % 3], B_tmp[m % 3]
                    P.stt("dve", T_[:], O_[:, m, :], gc[:, 8 + m:9 + m], rstd[:], ALU.mult, ALU.mult, [BO, B_gc, B_rstd], [BT])
                    P.tt("pool", XO[:, m, :], X[:, m, :], T_[:], ALU.add, [BX, BT], [BXO])
                d_ = P.dma("pool", x_dst[:, :, t0:t0 + NT], XO[:], r=[BXO])
                if last:
                    final_ops.append(d_)

            outproj(0)
            per_t = (len(achunks) + NTILE - 1) // NTILE
            for t in range(NTILE):
                if t + 1 < NTILE:
                    outproj(t + 1)
                for _ in range(per_t):
                    if achunks:
                        achunks.pop(0)()
                post(t)
            while achunks:
                achunks.pop(0)()
            while apend:
                apend.pop(0)()
        P.barrier()

    P.finish(final_ops)
    return nc


def prep_weights(inp, L, S):
    f32 = np.float32
    w_in = np.asarray(inp["w_in"], f32)[:L]
    sl = lambda a, b: w_in[:, :, a:b]
    kr = sl(1920, 1952)
    krx = np.concatenate([kr, kr[:, :, 16:32], kr[:, :, 0:16]], axis=2)
    wA = np.ascontiguousarray(np.concatenate(
        [sl(0, 512), sl(512, 1024), sl(1536, 1792), sl(1792, 1920), krx, sl(2464, 2976), sl(2976, 3488), sl(3488, 4000)], axis=2))
    wD = np.ascontiguousarray(np.concatenate([sl(1024, 1536), sl(1952, 2464), sl(4000, 4512), sl(4512, 7584)], axis=2))
    assert wA.shape[2] == W_A and wD.shape[2] == W_D
    uq = np.asarray(inp["mla_w_uq"], f32)[:L].reshape(L, 256, 8, 96)
    wuq = np.ascontiguousarray(np.concatenate([uq[..., 64:96], uq[..., 80:96], uq[..., 64:80], uq[..., 0:64]], axis=3).reshape(L, 256, 1024))
    wukv = np.ascontiguousarray(np.asarray(inp["mla_w_ukv"], f32)[:L])
    wbr = np.ascontiguousarray(np.asarray(inp["w_branch"], f32)[:L])
    wout = np.ascontiguousarray(np.asarray(inp["w_out"], f32)[:L])
    gcols = np.zeros((L, 128, 43), f32)
    gcols[:, :, 0:8] = np.asarray(inp["pre_g"], f32)[:L].reshape(L, 8, 128).transpose(0, 2, 1)
    gcols[:, :, 8:16] = np.asarray(inp["post_g"], f32)[:L].reshape(L, 8, 128).transpose(0, 2, 1)
    gcols[:, :, 16:40] = np.asarray(inp["gate_b"], f32)[:L].reshape(L, 3, 8, 128).transpose(0, 3, 1, 2).reshape(L, 128, 24)
    gcols[:, :, 40:42] = np.asarray(inp["mla_q_norm_g"], f32)[:L].reshape(L, 2, 128).transpose(0, 2, 1)
    gcols[:, :, 42] = np.asarray(inp["mla_kv_norm_g"], f32)[:L]
    lnrow = np.ascontiguousarray(np.stack([np.asarray(inp["sgu_ln_g"], f32)[:L], np.asarray(inp["sgu_ln_b"], f32)[:L]], axis=1))
    sb_ = np.asarray(inp["sgu_b"], f32)[:L]
    bsb = np.ascontiguousarray(np.repeat(sb_.reshape(L, 4, 2, 1, 128), 64, axis=3).transpose(0, 2, 3, 1, 4).reshape(L, 128, 4, 128))
    wsT = np.ascontiguousarray(np.asarray(inp["sgu_w"], f32)[:L].transpose(0, 3, 1, 2))
    rel = np.asarray(inp["ca_rel_bias"], f32)[:L]
    ki = np.arange(128)[:, None]
    cc = np.arange(RELW)[None, :]
    idx = np.clip(cc - 384 - ki, -128, 128) + 128
    relM = np.ascontiguousarray(rel[:, :, idx].transpose(0, 2, 1, 3))
    tri = (np.arange(128)[:, None] <= np.arange(128)[None, :]).astype(f32)
    half = 16
    inv = (np.float32(10000.0) ** (-np.arange(half, dtype=f32) / half)).astype(f32)
    ang = (np.arange(S, dtype=f32)[:, None] * inv[None, :]).astype(f32)
    cos = np.cos(ang).astype(f32).T
    sin = np.sin(ang).astype(f32).T
    c2 = np.concatenate([cos, cos], 0)
    s2 = np.concatenate([-sin, sin], 0)
    cs4 = np.ascontiguousarray(np.concatenate([c2, s2], 0))
    return dict(wA=wA, wD=wD, wuq=wuq, wukv=wukv, wbr=wbr, wout=wout, gcols=gcols, lnrow=lnrow, bsb=bsb, wsT=wsT,
                relM=relM, cs4=cs4, tri=tri)


_CORE_OF_BATCH = [0, 1, 4, 5]


def kernel(**inputs):
    x = np.asarray(inputs["x"], np.float32)
    B, S, _ = x.shape
    L = DEPTH
    w = prep_weights(inputs, L, S)
    nc = build_program(S, L)
    in_maps = []
    core_batch = {c: b for b, c in enumerate(_CORE_OF_BATCH)}
    zeros_x = np.zeros((D, S), np.float32)
    for core in range(8):
        m = dict(w)
        m["xT"] = np.ascontiguousarray(x[core_batch[core]].T) if core in core_batch else zeros_x
        in_maps.append(m)
    res = run_bass_kernel_spmd(nc, in_maps, core_ids=list(range(8)))
    out = np.empty((B, S, D), np.float32)
    for b, c in enumerate(_CORE_OF_BATCH):
        out[b] = res.results[c]["yout"].T
    return out
```

```python
import numpy as np
from contextlib import ExitStack
import concourse.bass as bass
import concourse.mybir as mybir
from concourse.bass_utils import run_bass_kernel_spmd

F32 = mybir.dt.float32
BF16 = mybir.dt.bfloat16
AF = mybir.ActivationFunctionType
ALU = mybir.AluOpType

D = 1024
DEPTH = 4
BATCH = 4
SEQ = 8192
EPS = 1e-6
NT = 512
W_A = 3008
W_D = 4608
RELW = 1152


class Buf:
    __slots__ = ("name", "last_w", "readers")

    def __init__(self, name=""):
        self.name = name
        self.last_w = None
        self.readers = []


class Op:
    __slots__ = ("eng", "fn", "deps", "dma", "sig", "cnt", "slot", "slotv")

    def __init__(self, eng, fn, dma):
        self.eng = eng
        self.fn = fn
        self.dma = dma
        self.deps = []
        self.sig = False
        self.cnt = 0
        self.slot = 0
        self.slotv = 0


ENGS = ("pe", "act", "dve", "pool", "sp")
STRICT_SAME_ENGINE = True
EPOCH = 30000
DMA_RING = 8


class Prog:
    def __init__(self, nc):
        self.nc = nc
        self.ops = []
        self.stack = ExitStack()
        self.last = {e: None for e in ENGS}
        self.dmas_since_bar = []

    def sb(self, stack, name, shape, dt):
        self.nsb = getattr(self, "nsb", 0) + 1
        return stack.enter_context(self.nc.sbuf_tensor(f"{name}_{self.nsb}", shape, dt))

    def op(self, eng, fn, reads=(), writes=(), dma=False):
        o = Op(eng, fn, dma)
        deps = {}
        for b in reads:
            if b.last_w is not None:
                deps[id(b.last_w)] = b.last_w
        for b in writes:
            if b.last_w is not None:
                deps[id(b.last_w)] = b.last_w
            for r in b.readers:
                deps[id(r)] = r
        for b in reads:
            b.readers.append(o)
        for b in writes:
            b.last_w = o
            b.readers = []
        for d in deps.values():
            if d is o:
                continue
            if (not d.dma) and d.eng == eng and (eng == "pe" or not STRICT_SAME_ENGINE):
                continue
            o.deps.append(d)
            d.sig = True
        self.ops.append(o)
        if dma:
            self.dmas_since_bar.append(o)
        else:
            self.last[eng] = o
        return o

    def barrier(self):
        deps = [o for o in self.last.values() if o is not None] + list(self.dmas_since_bar)
        for d in deps:
            d.sig = True
        for e in ENGS:
            o = Op(e, None, False)
            o.deps = list(deps)
            self.ops.append(o)
        self.dmas_since_bar = []

    def dma(self, eng, out, in_, r=(), w=()):
        return self.op(eng, lambda e: e.dma_start(out=out, in_=in_), r, w, dma=True)

    def mm(self, out, lhsT, rhs, start, stop, r, w):
        return self.op("pe", lambda e: e.matmul(out, lhsT, rhs, start=start, stop=stop), r, w)

    def act(self, out, in_, func, r, w, bias=None, scale=None):
        kw = {}
        if bias is not None:
            kw["bias"] = bias
        if scale is not None:
            kw["scale"] = scale
        return self.op("act", lambda e: e.activation(out=out, in_=in_, func=func, **kw), r, w)

    def copy(self, eng, out, in_, r, w):
        if eng == "act":
            return self.act(out, in_, AF.Copy, r, w)
        return self.op(eng, lambda e: e.tensor_copy(out=out, in_=in_), r, w)

    def tt(self, eng, out, in0, in1, op, r, w):
        return self.op(eng, lambda e: e.tensor_tensor(out=out, in0=in0, in1=in1, op=op), r, w)

    def ts(self, eng, out, in0, s1, s2, op0, op1, r, w):
        if op1 is None:
            return self.op(eng, lambda e: e.tensor_scalar(out=out, in0=in0, scalar1=s1, scalar2=None, op0=op0), r, w)
        return self.op(eng, lambda e: e.tensor_scalar(out=out, in0=in0, scalar1=s1, scalar2=s2, op0=op0, op1=op1), r, w)

    def stt(self, eng, out, in0, scalar, in1, op0, op1, r, w):
        return self.op(eng, lambda e: e.scalar_tensor_tensor(out=out, in0=in0, scalar=scalar, in1=in1, op0=op0, op1=op1), r, w)

    def recip(self, out, in_, r, w):
        return self.op("dve", lambda e: e.reciprocal(out=out, in_=in_), r, w)

    def bn_stats(self, out, in_, r, w):
        return self.op("dve", lambda e: e.bn_stats(out=out, in_=in_), r, w)

    def bn_aggr(self, out, in_, r, w):
        return self.op("dve", lambda e: e.bn_aggr(out=out, in_=in_), r, w)

    def memset(self, eng, ap, val, w):
        return self.op(eng, lambda e: e.memset(ap, val), (), w)

    def finish(self, final_ops):
        nc = self.nc
        for o in final_ops:
            o.sig = True
        cnt = {e: 0 for e in ENGS}
        dcnt = {e: 0 for e in ENGS}
        for o in self.ops:
            if o.dma:
                m = dcnt[o.eng]
                dcnt[o.eng] += 1
                o.slot = m % DMA_RING
                o.slotv = m // DMA_RING
            elif o.sig:
                cnt[o.eng] += 1
                o.cnt = cnt[o.eng]
        st = self.stack
        sems = {e: [st.enter_context(nc.semaphore(f"s_{e}{i}")) for i in range(cnt[e] // EPOCH + 1)] for e in ENGS}
        rings = {e: ([st.enter_context(nc.semaphore(f"d_{e}{i}")) for i in range(DMA_RING)] if dcnt[e] else [])
                 for e in ENGS}
        ops = self.ops

        def emit(engname, e):
            known = {}

            def wait(key, sem, val):
                if known.get(key, 0) >= val:
                    return
                known[key] = val
                e.wait_ge(sem, val)

            for o in ops:
                if o.eng != engname:
                    continue
                for d in o.deps:
                    if d.dma:
                        wait(("d", d.eng, d.slot), rings[d.eng][d.slot], 16 * (d.slotv + 1))
                    else:
                        ep = (d.cnt - 1) // EPOCH
                        wait(("c", d.eng, ep), sems[d.eng][ep], (d.cnt - 1) % EPOCH + 1)
                if o.fn is None:
                    continue
                if o.dma:
                    if o.slotv > 0:
                        wait(("d", o.eng, o.slot), rings[o.eng][o.slot], 16 * o.slotv)
                    o.fn(e).then_inc(rings[o.eng][o.slot], 16)
                else:
                    ins = o.fn(e)
                    if o.sig:
                        ins.then_inc(sems[o.eng][(o.cnt - 1) // EPOCH], 1)
            if engname == "sp":
                for d in final_ops:
                    wait(("d", d.eng, d.slot), rings[d.eng][d.slot], 16 * (d.slotv + 1))

        with nc.Block() as block:
            @block.tensor
            def _(e):
                emit("pe", e)

            @block.scalar
            def _(e):
                emit("act", e)

            @block.vector
            def _(e):
                emit("dve", e)

            @block.gpsimd
            def _(e):
                emit("pool", e)

            @block.sync
            def _(e):
                emit("sp", e)
        self.stack.close()


class Rot:
    def __init__(self, items):
        self.items = items
        self.i = 0

    def next(self):
        it = self.items[self.i % len(self.items)]
        self.i += 1
        return it


def build_program(S, L, dbg=False):
    nc = bass.Bass("TRN2", target_bir_lowering=False)
    NTILE = S // NT
    P = Prog(nc)

    def din(name, shape, dt=F32):
        return nc.dram_tensor(name, shape, dt, kind="ExternalInput").ap()

    def dscr(name, shape, dt):
        return nc.dram_tensor(name, shape, dt, kind=("ExternalOutput" if dbg else "Internal")).ap()

    xT = din("xT", [D, S])
    wA = din("wA", [L, D, W_A])
    wD = din("wD", [L, D, W_D])
    wuq = din("wuq", [L, 256, 1024])
    wukv = din("wukv", [L, 128, 1024])
    wbr = din("wbr", [L, 3, 512, D])
    wout = din("wout", [L, D, D])
    gcols = din("gcols", [L, 128, 43])
    lnrow = din("lnrow", [L, 2, 512])
    bsb = din("bsb", [L, 128, 4, 128])
    wsT = din("wsT", [L, 128, 8, 128])
    relM = din("relM", [L, 128, 8, RELW])
    cs4 = din("cs4", [64, S])
    tri = din("tri", [128, 128])
    yout = nc.dram_tensor("yout", [D, S], F32, kind="ExternalOutput").ap()

    XN = dscr("XN", [128, 8, S], BF16)
    XR = dscr("XR", [128, 8, S], F32)
    YA = dscr("YA", [128, 4, S], BF16)
    YB = dscr("YB", [128, 4, S], BF16)
    YC = dscr("YC", [128, 4, S], BF16)
    CQ = dscr("CQ", [128, 2, S], BF16)
    CKV = dscr("CKV", [128, S], BF16)
    KR = dscr("KR", [32, S], BF16)
    QC = dscr("QC", [128, 4, S], BF16)
    KC = dscr("KC", [128, 4, S], BF16)
    VC = dscr("VC", [S, 1024], BF16)
    MG = dscr("MG", [128, 8, S], BF16)
    DBG = dscr("DBG", [128, 2, S], F32) if dbg else None

    xT_v = xT.rearrange("(c p) s -> p c s", p=128)
    yout_v = yout.rearrange("(c p) s -> p c s", p=128)
    VC_v = VC.rearrange("(n p) f -> p n f", p=128)

    psum = [P.stack.enter_context(nc.psum_tensor(f"ps{i}", [128, 512], F32)) for i in range(8)]
    psB = [Buf(f"ps{i}") for i in range(8)]

    ones_bf = P.sb(P.stack, "ones_bf", [128, 128], BF16)
    B_ones = Buf("ones")
    P.memset("pool", ones_bf[:], 1.0, [B_ones])

    final_ops = []

    def load_chunks(dst, dstB, src_rows, ncols, stage, stageB, k, piece, cast_eng=None, pend=None):
        out = []
        off = 0
        while off < ncols:
            n = min(piece, ncols - off)

            def f(off=off, n=n):
                sl = k[0] % 2
                k[0] += 1
                if pend is None:
                    P.dma("sp", stage[sl][:, 0:n], src_rows[:, off:off + n], w=[stageB[sl]])
                    eng = cast_eng if cast_eng else ("dve" if k[0] % 2 else "act")
                    P.copy(eng, dst[:, off:off + n], stage[sl][:, 0:n], [stageB[sl]], [dstB])
                else:
                    P.dma("pool", stage[sl][:, 0:n], src_rows[:, off:off + n], w=[stageB[sl]])
                    while pend:
                        pend.pop(0)()
                    pend.append(lambda: P.copy(cast_eng, dst[:, off:off + n], stage[sl][:, 0:n], [stageB[sl]], [dstB]))
            out.append(f)
            off += n
        return out

    carryA = ExitStack()
    wA_sb = P.sb(carryA, "wA_sb", [128, 8, W_A], BF16)
    B_wA = Buf()
    with ExitStack() as tmpst:
        stage0 = [P.sb(tmpst, f"stage0{i}", [128, 1504], F32) for i in range(2)]
        stage0B = [Buf(), Buf()]
        k0 = [0]
        for c in range(8):
            for ch in load_chunks(wA_sb[:, c, :], B_wA, wA[0, c * 128:(c + 1) * 128, :], W_A, stage0, stage0B, k0, 1504):
                ch()
    P.barrier()

    for l in range(L):
        last = (l == L - 1)
        x_src = xT_v if l == 0 else XR
        x_dst = yout_v if last else XR

        with ExitStack() as st:
            gc = P.sb(st, "gcA", [128, 43], F32)
            B_gc = Buf()
            lng = P.sb(st, "lng", [128, 512], F32)
            lnb = P.sb(st, "lnb", [128, 512], F32)
            B_ln = Buf()
            bsb_sb = P.sb(st, "bsb_sb", [128, 4, 128], F32)
            B_bsb = Buf()
            ws_f = P.sb(st, "ws_f", [128, 8, 128], F32)
            ws_sb = P.sb(st, "ws_sb", [128, 8, 128], BF16)
            tri_sb = P.sb(st, "tri_sb", [128, 128], F32)
            B_ws = Buf()
            xt = [P.sb(st, f"xt{i}", [128, 8, NT], F32) for i in range(2)]
            B_xt = [Buf(), Buf()]
            sq = P.sb(st, "sqA", [128, 8, NT], BF16)
            B_sq = Buf()
            rstd = P.sb(st, "rstdA", [128, NT], F32)
            B_rstd = Buf()
            xn = [P.sb(st, f"xnA{i}", [128, 8, NT], BF16) for i in range(2)]
            B_xn = [Buf(), Buf()]
            uA2 = [P.sb(st, f"uA{i}", [128, 4, NT], BF16) for i in range(2)]
            B_uA2 = [Buf(), Buf()]
            qd_sb = P.sb(st, "qd_sb", [128, 3, NT], F32)
            qsq = P.sb(st, "qsq", [128, 3, NT], BF16)
            B_qd = Buf()
            B_qsq = Buf()
            rq = P.sb(st, "rq", [128, 2, NT], F32)
            B_rq = Buf()
            rq2 = P.sb(st, "rq2", [128, 2, NT], F32)
            B_rq2 = Buf()
            cq_sb = P.sb(st, "cq_sbA", [128, 3, NT], BF16)
            B_cq = Buf()
            cst = P.sb(st, "cstA", [64, NT], F32)
            B_cst = Buf()
            krt = P.sb(st, "krt", [64, NT], F32)
            kr_bf = P.sb(st, "kr_bf", [32, NT], BF16)
            krs = P.sb(st, "krs", [32, NT], F32)
            B_kr = Buf()
            qc_sb = P.sb(st, "qc_sbA", [128, 4, NT], BF16)
            kc_sb = P.sb(st, "kc_sbA", [128, 4, NT], BF16)
            B_qc = Buf()
            B_kc = Buf()
            vaug = [P.sb(st, f"vaugA{i}", [128, 4, 1024], BF16) for i in range(2)]
            B_vaug = [Buf(), Buf()]
            vn0 = [P.sb(st, f"vn0_{i}", [128, NT], F32) for i in range(2)]
            vn1 = [P.sb(st, f"vn1_{i}", [128, NT], F32) for i in range(2)]
            B_vn1 = [Buf(), Buf()]
            vn = [P.sb(st, f"vn{i}", [128, NT], BF16) for i in range(2)]
            B_vn0 = [Buf(), Buf()]
            B_vn = [Buf(), Buf()]
            stt6 = P.sb(st, "bnst", [128, 4, 6], F32)
            mv = P.sb(st, "bnmv", [128, 4, 4], F32)
            B_mv = Buf()
            ytmp2 = [P.sb(st, f"ytmp{i}", [128, 4, 128], F32) for i in range(2)]
            B_ytmp2 = [Buf(), Buf()]
            yA_sb = P.sb(st, "yA_sb", [128, 4, NT], BF16)
            B_yA = Buf()

            P.dma("sp", gc[:], gcols[l], w=[B_gc])
            P.dma("sp", lng[:], lnrow[l, 0:1, :].partition_broadcast(128), w=[B_ln])
            P.dma("sp", lnb[:], lnrow[l, 1:2, :].partition_broadcast(128), w=[B_ln])
            P.dma("sp", bsb_sb[:], bsb[l], w=[B_bsb])
            P.dma("sp", ws_f[:], wsT[l], w=[B_ws])
            P.dma("sp", tri_sb[:], tri[:, :], w=[B_ws])
            for g in range(8):
                P.tt("dve", ws_sb[:, g, :], ws_f[:, g, :], tri_sb[:], ALU.mult, [], [B_ws])
            for i in range(2):
                va = vaug[i][:].rearrange("p j (hp two e) -> p j hp two e", two=2, e=128)
                for j in range(4):
                    P.memset("pool", va[:, j, :, 0, 64:128], 1.0, [B_vaug[i]])
                    P.memset("pool", va[:, j, :, 1, 0:64], 1.0, [B_vaug[i]])

            rot = Rot([(psum[i], psB[i]) for i in range(1, 8)])
            vnb = [P.sb(st, f"vnb{i}", [128, NT], BF16) for i in range(2)]
            B_vnb = [Buf(), Buf()]
            vraw = [P.sb(st, f"vraw{i}", [128, NT], F32) for i in range(4)]
            B_vraw = [Buf() for _ in range(4)]
            vn4 = [vn[0], vn[1], vnb[0], vnb[1]]
            B_vn4 = [B_vn[0], B_vn[1], B_vnb[0], B_vnb[1]]

            def normA(t):
                t0 = t * NT
                X, BX = xt[t % 2], B_xt[t % 2]
                XNt, BXN = xn[t % 2], B_xn[t % 2]
                P.dma("sp", X[:], x_src[:, :, t0:t0 + NT], w=[BX])
                for c in range(8):
                    P.act(sq[:, c, :], X[:, c, :], AF.Square, [BX], [B_sq])
                for c in range(8):
                    P.mm(psum[0][:], ones_bf[:], sq[:, c, :], c == 0, c == 7, [B_ones, B_sq], [psB[0]])
                P.act(rstd[:], psum[0][:], AF.Ln, [], [psB[0], B_rstd], bias=EPS, scale=1.0 / D)
                P.act(rstd[:], rstd[:], AF.Exp, [], [B_rstd], scale=-0.5)
                for c in range(8):
                    P.stt("dve", XNt[:, c, :], X[:, c, :], gc[:, c:c + 1], rstd[:],
                          ALU.mult, ALU.mult, [BX, B_gc, B_rstd], [BXN])
                P.dma("pool", XN[:, :, t0:t0 + NT], XNt[:], r=[BXN])

            def projA(t):
                t0 = t * NT
                XNt, BXN = xn[t % 2], B_xn[t % 2]
                uA, B_uA = uA2[t % 2], B_uA2[t % 2]
                P.dma("sp", cst[:], cs4[0:64, t0:t0 + NT], w=[B_cst])

                def fm(off, M):
                    ps, pb = rot.next()
                    for c in range(8):
                        P.mm(ps[0:M, :], wA_sb[:, c, off:off + M], XNt[:, c, :], c == 0, c == 7, [B_wA, BXN], [pb])
                    return ps, pb

                vps = []
                for j in range(4):
                    ps, pb = rot.next()
                    for c in range(8):
                        P.mm(ps[:], XNt[:, c, j * 128:(j + 1) * 128], wA_sb[:, c, 512:1024], c == 0, c == 7,
                             [B_wA, BXN], [pb])
                    P.copy("act", vraw[j][:], ps[:], [], [pb, B_vraw[j]])
                for j in range(4):
                    ps, pb = fm(j * 128, 128)
                    P.copy("act", uA[:, j, :], ps[:], [], [pb, B_uA])
                for j in range(4):
                    P.bn_stats(stt6[:, j, :], vraw[j][:], [B_vraw[j]], [B_mv])
                    P.bn_aggr(mv[:, j, 0:2], stt6[:, j, :], [], [B_mv])
                for c3 in range(3):
                    ps, pb = fm(1024 + c3 * 128, 128)
                    P.copy("act", qd_sb[:, c3, :], ps[:], [], [pb, B_qd])
                    P.act(qsq[:, c3, :], ps[:], AF.Square, [], [pb, B_qsq])
                P.act(mv[:, :, 2], mv[:, :, 1], AF.Ln, [], [B_mv], bias=EPS, scale=1.0)
                ps, pb = fm(1408, 64)
                P.tt("dve", krt[32:64, :], ps[32:64, :], cst[32:64, :], ALU.mult, [B_cst], [pb, B_kr])
                P.tt("dve", krt[0:32, :], ps[0:32, :], cst[0:32, :], ALU.mult, [B_cst], [pb, B_kr])
                P.act(mv[:, :, 3], mv[:, :, 2], AF.Exp, [], [B_mv], scale=-0.5)
                P.stt("dve", mv[:, :, 2], mv[:, :, 0], -1.0, mv[:, :, 3], ALU.mult, ALU.mult, [], [B_mv])
                for cg in range(4):
                    ps, pb = fm(1472 + cg * 128, 128)
                    P.copy("act", qc_sb[:, cg, :], ps[:], [], [pb, B_qc])
                P.dma("pool", QC[:, :, t0:t0 + NT], qc_sb[:], r=[B_qc])
                P.copy("act", krs[0:32, :], krt[32:64, :], [], [B_kr])
                for j in range(4):
                    P.act(vn0[j % 2][:], vraw[j][:], AF.Identity, [B_mv, B_vraw[j]], [B_vn0[j % 2]], bias=mv[:, j, 2:3], scale=mv[:, j, 3:4])
                    P.tt("dve", vn1[j % 2][:], vn0[j % 2][:], lng[:], ALU.mult, [B_ln, B_vn0[j % 2]], [B_vn1[j % 2]])
                    P.tt("pool", vn4[j][:], vn1[j % 2][:], lnb[:], ALU.add, [B_ln, B_vn1[j % 2]], [B_vn4[j]])
                P.tt("pool", kr_bf[:], krt[0:32, :], krs[0:32, :], ALU.add, [], [B_kr])
                P.dma("pool", KR[:, t0:t0 + NT], kr_bf[:], r=[B_kr])
                for cg in range(4):
                    ps, pb = fm(1984 + cg * 128, 128)
                    P.copy("dve" if cg % 2 else "act", kc_sb[:, cg, :], ps[:], [], [pb, B_kc])
                P.dma("pool", KC[:, :, t0:t0 + NT], kc_sb[:], r=[B_kc])
                ps, pb = rot.next()
                P.mm(ps[:], ones_bf[:], qsq[:, 0, :], True, False, [B_ones, B_qsq], [pb])
                P.mm(ps[:], ones_bf[:], qsq[:, 1, :], False, True, [B_ones, B_qsq], [pb])
                P.act(rq[:, 0, :], ps[:], AF.Ln, [], [pb, B_rq], bias=EPS, scale=1.0 / 256)
                ps, pb = rot.next()
                P.mm(ps[:], ones_bf[:], qsq[:, 2, :], True, True, [B_ones, B_qsq], [pb])
                P.act(rq[:, 1, :], ps[:], AF.Ln, [], [pb, B_rq], bias=EPS, scale=1.0 / 128)
                VA_ = vaug[t % 2]
                BVA = B_vaug[t % 2]
                va = VA_[:].rearrange("p j (hp two e) -> p j hp two e", two=2, e=128)
                for j in range(4):
                    ps, pb = rot.next()
                    for c in range(8):
                        P.mm(ps[:], XNt[:, c, j * 128:(j + 1) * 128], wA_sb[:, c, 2496:3008], c == 0, c == 7,
                             [B_wA, BXN], [pb])
                    pv = ps[:].rearrange("p (hp two d) -> p hp two d", two=2, d=64)
                    P.copy("act", va[:, j, :, 0, 0:64], pv[:, :, 0, :], [], [pb, BVA])
                    P.copy("dve", va[:, j, :, 1, 64:128], pv[:, :, 1, :], [], [pb, BVA])
                P.dma("pool", VC_v[:, 4 * t:4 * t + 4, :], VA_[:], r=[BVA])
                P.act(rq2[:], rq[:], AF.Exp, [B_rq], [B_rq2], scale=-0.5)
                for c3 in range(3):
                    P.stt("dve", cq_sb[:, c3, :], qd_sb[:, c3, :], gc[:, 40 + c3:41 + c3],
                          rq2[:, 0 if c3 < 2 else 1, :], ALU.mult, ALU.mult, [B_qd, B_gc, B_rq2], [B_cq])
                P.dma("pool", CQ[:, :, t0:t0 + NT], cq_sb[:, 0:2, :], r=[B_cq])
                P.dma("pool", CKV[:, t0:t0 + NT], cq_sb[:, 2, :], r=[B_cq])
                for j in range(4):
                    ps2, pb2 = rot.next()
                    for g in range(8):
                        hp = (g % 2) * 64
                        cg = g // 2
                        P.mm(ps2[hp:hp + 64, cg * 128:(cg + 1) * 128], vn4[j][:, g * 64:(g + 1) * 64], ws_sb[:, g, :],
                             True, True, [B_vn4[j], B_ws], [pb2])
                    ytmp, B_ytmp = ytmp2[j % 2], B_ytmp2[j % 2]
                    P.tt("dve", ytmp[:], ps2[:].rearrange("p (c t) -> p c t", t=128), bsb_sb[:], ALU.add,
                         [B_bsb], [pb2, B_ytmp])
                    P.tt("pool", yA_sb[:, :, j * 128:(j + 1) * 128], ytmp[:], uA[:, :, j * 128:(j + 1) * 128],
                         ALU.mult, [B_ytmp, B_uA], [B_yA])
                P.dma("pool", YA[:, :, t0:t0 + NT], yA_sb[:], r=[B_yA])

            normA(0)
            for t in range(NTILE):
                if t + 1 < NTILE:
                    normA(t + 1)
                projA(t)
        carryA.close()
        P.barrier()

        with ExitStack() as st:
            ckv_sb = P.sb(st, "ckv_sb", [128, S], BF16)
            cqr = P.sb(st, "cqr", [128, 2, S], BF16)
            B_ckv = Buf()
            B_cqr = Buf()
            kT = [P.sb(st, f"kT{i}", [128, S], BF16) for i in range(2)]
            B_kT = [Buf(), Buf()]
            qT = [P.sb(st, f"qT{i}", [128, S], BF16) for i in range(2)]
            B_qT = [Buf(), Buf()]
            vT = [P.sb(st, f"vT{i}", [128, S // 128, 128], BF16) for i in range(2)]
            B_vT = [Buf(), Buf()]
            yb_sb = [P.sb(st, f"yb_sb{i}", [128, S], BF16) for i in range(2)]
            B_yb = [Buf(), Buf()]
            wuq_sb = P.sb(st, "wuq_sb", [128, 2, 1024], BF16)
            wukv_sb = P.sb(st, "wukv_sb", [128, 1024], BF16)
            B_wC = Buf()
            stage = [P.sb(st, f"stageC{i}", [128, 512], F32) for i in range(2)]
            stageB = [Buf(), Buf()]
            cst = [P.sb(st, f"cstC{i}", [64, NT], F32) for i in range(2)]
            B_cst = [Buf(), Buf()]
            t12 = P.sb(st, "t12", [64, NT], F32)
            t3 = P.sb(st, "t3", [32, NT], F32)
            B_t12 = Buf()
            NPC = 6
            p_sb = [P.sb(st, f"p_sbC{i}", [128, NT], BF16) for i in range(NPC)]
            B_p = [Buf() for _ in range(NPC)]
            pd = [P.sb(st, f"pdC{i}", [128, NT], BF16) for i in range(4)]
            B_pd = [Buf() for _ in range(4)]
            for i in range(4):
                P.memset("pool", pd[i][:], 0.0, [B_pd[i]])
            rec = P.sb(st, "recC", [128, NT], F32)
            B_rec = Buf()
            k = [0]
            for kc in range(2):
                for ch in load_chunks(wuq_sb[:, kc, :], B_wC, wuq[l, kc * 128:(kc + 1) * 128, :], 1024, stage, stageB, k, 512):
                    ch()
            for ch in load_chunks(wukv_sb[:, :], B_wC, wukv[l, :, :], 1024, stage, stageB, k, 512):
                ch()
            P.dma("sp", ckv_sb[:], CKV[:, :], w=[B_ckv])
            for kc in range(2):
                P.dma("sp", cqr[:, kc, :], CQ[:, kc, :], w=[B_cqr])
            for i in range(2):
                P.dma("sp", kT[i][0:32, :], KR[:, :], w=[B_kT[i]])
                P.memset("pool", kT[i][32:64, :], 0.0, [B_kT[i]])
                P.memset("pool", qT[i][32:64, :], 0.0, [B_qT[i]])
            P.memset("pool", vT[0][:, :, 64:128], 1.0, [B_vT[0]])
            P.memset("pool", vT[1][:, :, 0:64], 1.0, [B_vT[1]])
            grot = Rot([(psum[i], psB[i]) for i in range(0, 2)])
            srot = Rot([(psum[i], psB[i]) for i in range(2, 6)])
            orot = Rot([(psum[i], psB[i]) for i in range(6, 8)])
            scale = 96 ** -0.5
            cctr = [0]

            def gen_chunks(h):
                sl = h % 2
                K_, BK = kT[sl], B_kT[sl]
                Q_, BQ = qT[sl], B_qT[sl]
                V_, BV = vT[sl], B_vT[sl]
                voff = 0 if sl == 0 else 64
                out = []

                def kchunk(tt_):
                    ps, pb = grot.next()
                    P.mm(ps[64:128, :], wukv_sb[:, h * 128:h * 128 + 64], ckv_sb[:, tt_ * NT:(tt_ + 1) * NT], True, True,
                         [B_wC, B_ckv], [pb])
                    P.copy("dve", K_[64:128, tt_ * NT:(tt_ + 1) * NT], ps[64:128, :], [], [pb, BK])

                def vchunk(t8):
                    ps, pb = grot.next()
                    for jj in range(8):
                        kt = t8 * 8 + jj
                        P.mm(ps[:, jj * 64:(jj + 1) * 64], ckv_sb[:, kt * 128:(kt + 1) * 128],
                             wukv_sb[:, h * 128 + 64:h * 128 + 128], True, True, [B_wC, B_ckv], [pb])
                    P.copy("dve", V_[:, t8 * 8:(t8 + 1) * 8, voff:voff + 64],
                           ps[:].rearrange("p (j d) -> p j d", d=64), [], [pb, BV])

                def qchunk(tt_):
                    CS, BCS = cst[cctr[0] % 2], B_cst[cctr[0] % 2]
                    cctr[0] += 1
                    P.dma("sp", CS[:], cs4[0:64, tt_ * NT:(tt_ + 1) * NT], w=[BCS])
                    ps, pb = grot.next()
                    for kc in range(2):
                        P.mm(ps[:], wuq_sb[:, kc, h * 128:(h + 1) * 128], cqr[:, kc, tt_ * NT:(tt_ + 1) * NT], kc == 0, kc == 1,
                             [B_wC, B_cqr], [pb])
                    P.tt("dve", t12[0:64, :], ps[0:64, :], CS[0:64, :], ALU.mult, [BCS], [pb, B_t12])
                    P.copy("dve", Q_[64:128, tt_ * NT:(tt_ + 1) * NT], ps[64:128, :], [], [pb, BQ])
                    P.copy("act", t3[0:32, :], t12[32:64, :], [], [B_t12])
                    P.tt("pool", Q_[0:32, tt_ * NT:(tt_ + 1) * NT], t12[0:32, :], t3[0:32, :], ALU.add, [], [B_t12, BQ])

                for tt_ in range(NTILE):
                    out.append(lambda tt_=tt_: kchunk(tt_))
                for t8 in range(S // 1024):
                    out.append(lambda t8=t8: vchunk(t8))
                for tt_ in range(NTILE):
                    out.append(lambda tt_=tt_: qchunk(tt_))
                return out

            for ch in gen_chunks(0):
                ch()
            LA = 5
            it = 0
            for h in range(8):
                sl = h % 2
                c = h // 2
                K_, BK = kT[sl], B_kT[sl]
                Q_, BQ = qT[sl], B_qT[sl]
                V_, BV = vT[sl], B_vT[sl]
                Y_, BY = yb_sb[c % 2], B_yb[c % 2]
                nxt = gen_chunks(h + 1) if h + 1 < 8 else []
                items = [(g, j) for g in range(NTILE) for j in range(4 * g + 4)]
                every = max(1, len(items) // (len(nxt) + 1)) if nxt else 0
                pend = []
                obank = {}

                def emit_pv(item):
                    g, j, c0, PT, BP = item
                    nj = 4 * g + 4
                    if j == 0:
                        obank[g] = orot.next()
                    ops_, opb = obank[g]
                    P.mm(ops_[:, c0:NT], V_[:, j, :], PT[:, c0:NT], j == 0, j == nj - 1, [BV, BP], [opb])
                    if j == nj - 1:
                        if sl == 0:
                            P.recip(rec[64:128, :], ops_[64:128, :], [], [opb, B_rec])
                            P.tt("dve", Y_[0:64, g * NT:(g + 1) * NT], ops_[0:64, :], rec[64:128, :], ALU.mult, [B_rec], [opb, BY])
                        else:
                            P.recip(rec[0:64, :], ops_[0:64, :], [], [opb, B_rec])
                            P.tt("dve", Y_[64:128, g * NT:(g + 1) * NT], ops_[64:128, :], rec[0:64, :], ALU.mult, [B_rec], [opb, BY])
                        del obank[g]

                for idx, (g, j) in enumerate(items):
                    r_ = j - 4 * g
                    c0 = 128 * r_ if r_ > 0 else 0
                    sps, spb = srot.next()
                    P.mm(sps[:, c0:NT], K_[:, j * 128:(j + 1) * 128], Q_[:, g * NT + c0:(g + 1) * NT], True, True,
                         [BK, BQ], [spb])
                    if r_ >= 0:
                        PT, BP = pd[r_], B_pd[r_]
                    else:
                        PT, BP = p_sb[it % NPC], B_p[it % NPC]
                        it += 1
                    P.act(PT[:, c0:NT], sps[:, c0:NT], AF.Exp, [], [spb, BP], scale=scale)
                    if r_ >= 0:
                        P.memset("pool", PT[64:128, c0:c0 + 64], 0.0, [BP])
                    pend.append((g, j, c0, PT, BP))
                    if len(pend) > LA:
                        emit_pv(pend.pop(0))
                    if nxt and every and (idx % every == every - 1):
                        nxt.pop(0)()
                while pend:
                    emit_pv(pend.pop(0))
                while nxt:
                    nxt.pop(0)()
                if sl == 1:
                    P.dma("pool", YB[:, c, :], Y_[:], r=[BY])
        P.barrier()

        carryD = ExitStack()
        wD_sb = P.sb(carryD, "wD_sb", [128, 8, W_D], BF16)
        wbr_sb = P.sb(carryD, "wbr_sb", [128, 12, D], BF16)
        B_wD = Buf()
        stageD = [P.sb(carryD, f"stageD{i}", [128, 1152], F32) for i in range(2)]
        stageDB = [Buf(), Buf()]
        kD = [0]
        dchunks = []
        dpend = []
        for c in range(8):
            dchunks += load_chunks(wD_sb[:, c, :], B_wD, wD[l, c * 128:(c + 1) * 128, :], W_D, stageD, stageDB, kD, 1152, "pool", dpend)
        for i in range(3):
            for kc in range(4):
                dchunks += load_chunks(wbr_sb[:, i * 4 + kc, :], B_wD, wbr[l, i, kc * 128:(kc + 1) * 128, :], D, stageD, stageDB, kD,
                                       1024, "pool", dpend)
        with ExitStack() as st:
            rel_sb = P.sb(st, "rel_sb", [128, 8, RELW], F32)
            B_rel = Buf()
            P.dma("sp", rel_sb[:, 0:4, :], relM[l, :, 0:4, :], w=[B_rel])
            P.dma("sp", rel_sb[:, 4:8, :], relM[l, :, 4:8, :], w=[B_rel])
            qct = [P.sb(st, f"qct{i}", [128, 4, NT], BF16) for i in range(2)]
            B_qct = [Buf(), Buf()]
            kct = [P.sb(st, f"kct{i}", [128, 4, NT], BF16) for i in range(3)]
            B_kct = [Buf(), Buf(), Buf()]
            vt = [P.sb(st, f"vtB{i}", [128, 4, 1024], BF16) for i in range(3)]
            B_vt = [Buf(), Buf(), Buf()]
            NSC, NPB = 6, 8
            sc_sb = [P.sb(st, f"sc_sb{i}", [128, NT], BF16) for i in range(NSC)]
            cbias = P.sb(st, "cbias", [128, 8], F32)
            B_cb = Buf()
            P.copy("dve", cbias[:], rel_sb[:, :, RELW - 1], [B_rel], [B_cb])
            for h_ in range(8):
                P.act(rel_sb[:, h_, :], rel_sb[:, h_, :], AF.Exp, [B_cb], [B_rel])
            B_sc = [Buf() for _ in range(NSC)]
            p_sb = [P.sb(st, f"p_sbB{i}", [128, NT], BF16) for i in range(NPB)]
            B_p = [Buf() for _ in range(NPB)]
            for i in range(NPB):
                P.memset("pool", p_sb[i][:], 0.0, [B_p[i]])
            rec2 = [P.sb(st, f"recB{i}", [128, NT], F32) for i in range(2)]
            B_rec2 = [Buf(), Buf()]
            yc_sb = [P.sb(st, "yc_sb0", [128, 4, NT], BF16)] * 2
            B_yc = [Buf()] * 2
            srot = Rot([(psum[i], psB[i]) for i in range(0, 5)])
            orot = Rot([(psum[i], psB[i]) for i in range(5, 8)])
            scale = 64 ** -0.5
            LA = 6
            pend = []
            obank = {}

            def emit_pv(item):
                g, h, j, js, src, sub, c0, c1, PT, BP = item
                c = h // 2
                Y, BY = yc_sb[g % 2], B_yc[g % 2]
                if j == js[0]:
                    obank[(g, h)] = orot.next()
                ops_, opb = obank[(g, h)]
                pc0 = c0 if j >= 5 else 0
                P.mm(ops_[:, pc0:NT], vt[src][:, sub, h * 128:(h + 1) * 128], PT[:, pc0:NT], j == js[0], j == js[-1],
                     [B_vt[src], BP], [opb])
                if j == js[-1]:
                    RC, BRC = rec2[h % 2], B_rec2[h % 2]
                    if h % 2 == 0:
                        P.act(RC[64:128, :], ops_[64:128, :], AF.Ln, [], [opb, BRC])
                        P.act(RC[64:128, :], RC[64:128, :], AF.Exp, [], [BRC], scale=-1.0)
                        P.tt("dve", Y[0:64, c, :], ops_[0:64, :], RC[64:128, :], ALU.mult, [BRC], [opb, BY])
                    else:
                        P.act(RC[0:64, :], ops_[0:64, :], AF.Ln, [], [opb, BRC])
                        P.act(RC[0:64, :], RC[0:64, :], AF.Exp, [], [BRC], scale=-1.0)
                        P.tt("dve", Y[64:128, c, :], ops_[64:128, :], RC[0:64, :], ALU.mult, [BRC], [opb, BY])
                    del obank[(g, h)]
                    if h == 7:
                        P.dma("pool", YC[:, :, g * NT:(g + 1) * NT], Y[:], r=[BY])

            it = 0
            n_items = 8 * (8 * NTILE - 4)
            every_d = max(1, n_items // (len(dchunks) + 1))
            for g in range(NTILE):
                t0 = g * NT
                Q = qct[g % 2]
                BQ = B_qct[g % 2]
                P.dma("sp", Q[:], QC[:, :, t0:t0 + NT], w=[BQ])
                P.dma("sp", kct[g % 3][:], KC[:, :, t0:t0 + NT], w=[B_kct[g % 3]])
                P.dma("sp", vt[g % 3][:], VC_v[:, 4 * g:4 * g + 4, :], w=[B_vt[g % 3]])
                for h in range(8):
                    c = h // 2
                    hp = (h % 2) * 64
                    js = list(range(4, 8)) if g == 0 else list(range(8))
                    for j in js:
                        src = (g - 1) % 3 if j < 4 else g % 3
                        sub = j % 4
                        c0 = 64 * max(0, 2 * j - 8)
                        c1 = 64 * (min(7, 2 * j + 1) + 1)
                        cstart = 640 if j <= 2 else 512 - 128 * (j - 3)
                        sps, spb = srot.next()
                        P.mm(sps[:, c0:c1], kct[src][hp:hp + 64, c, sub * 128:(sub + 1) * 128], Q[hp:hp + 64, c, c0:c1],
                             True, True, [B_kct[src], BQ], [spb])
                        SC, BSC = sc_sb[it % NSC], B_sc[it % NSC]
                        PT, BP = p_sb[j], B_p[j]
                        it += 1
                        if j <= 2:
                            P.act(PT[:, c0:c1], sps[:, c0:c1], AF.Exp, [B_cb], [spb, BP], bias=cbias[:, h:h + 1], scale=scale)
                            meng = "pool"
                        else:
                            P.act(SC[:, c0:c1], sps[:, c0:c1], AF.Exp, [], [spb, BSC], scale=scale)
                            meng = "dve" if it % 3 else "pool"
                            P.tt(meng, PT[:, c0:c1], SC[:, c0:c1], rel_sb[:, h, cstart + c0:cstart + c1], ALU.mult,
                                 [B_rel, BSC], [BP])
                        if j <= 3:
                            P.memset(meng, PT[0:64, c1 - 64:c1], 0.0, [BP])
                        else:
                            P.memset(meng, PT[64:128, c0:c0 + 64], 0.0, [BP])
                        pend.append((g, h, j, js, src, sub, c0, c1, PT, BP))
                        while len(pend) > (3 if g == 0 else LA):
                            emit_pv(pend.pop(0))
                        if dchunks and it % every_d == 0:
                            dchunks.pop(0)()
            while pend:
                emit_pv(pend.pop(0))
            while dchunks:
                dchunks.pop(0)()
            while dpend:
                dpend.pop(0)()
        P.barrier()

        with ExitStack() as st:
            gc = P.sb(st, "gcD", [128, 43], F32)
            B_gc = Buf()
            xn = [P.sb(st, f"xnD{i}", [128, 8, NT], BF16) for i in range(2)]
            B_xn = [Buf(), Buf()]
            ys = [[P.sb(st, f"yD{i}_{b}", [128, 4, NT], BF16) for b in range(2)] for i in range(3)]
            B_ys = [[Buf(), Buf()] for _ in range(3)]
            yz2 = [[P.sb(st, f"yz{i}_{b}", [128, 4, NT], BF16) for i in range(3)] for b in range(2)]
            B_yz2 = [[Buf() for _ in range(3)] for _ in range(2)]
            sz = [P.sb(st, f"sz{i}", [128, NT], F32) for i in range(2)]
            B_sz = [Buf(), Buf()]
            gsb = [P.sb(st, f"gsb{i}", [128, NT], F32) for i in range(3)]
            B_gsb = [Buf() for _ in range(3)]
            acc = [P.sb(st, f"acc{i}", [128, NT], F32) for i in range(2)]
            B_acc = [Buf(), Buf()]
            tmp = [P.sb(st, f"tmpD{i}", [128, NT], F32) for i in range(2)]
            B_tmp = [Buf(), Buf()]
            mg = [P.sb(st, f"mg{i}", [128, 8, NT], BF16) for i in range(2)]
            B_mg = [Buf(), Buf()]
            P.dma("sp", gc[:], gcols[l], w=[B_gc])
            rot = Rot([(psum[i], psB[i]) for i in range(8)])
            ysrc = [YA, YB, YC]
            si = 0
            gi = 0
            ti = 0
            for t in range(NTILE):
                t0 = t * NT
                XNt, BXN = xn[t % 2], B_xn[t % 2]
                yz, B_yz = yz2[t % 2], B_yz2[t % 2]
                P.dma("sp", XNt[:], XN[:, :, t0:t0 + NT], w=[BXN])
                for i in range(3):
                    P.dma("sp", ys[i][t % 2][:], ysrc[i][:, :, t0:t0 + NT], w=[B_ys[i][t % 2]])
                for i in range(3):
                    for c in range(4):
                        ps, pb = rot.next()
                        for kc in range(8):
                            P.mm(ps[:], wD_sb[:, kc, i * 512 + c * 128:i * 512 + (c + 1) * 128], XNt[:, kc, :], kc == 0, kc == 7,
                                 [B_wD, BXN], [pb])
                        SZ, BSZ = sz[si % 2], B_sz[si % 2]
                        si += 1
                        P.act(SZ[:], ps[:], AF.Silu, [], [pb, BSZ])
                        P.tt("dve" if si % 2 else "pool", yz[i][:, c, :], ys[i][t % 2][:, c, :], SZ[:], ALU.mult,
                             [B_ys[i][t % 2], BSZ], [B_yz[i]])
                MGt, BMG = mg[t % 2], B_mg[t % 2]
                for m in range(8):
                    AC, BAC = acc[m % 2], B_acc[m % 2]
                    for i in range(3):
                        ps, pb = rot.next()
                        off = 1536 + i * 1024 + m * 128
                        for kc in range(8):
                            P.mm(ps[:], wD_sb[:, kc, off:off + 128], XNt[:, kc, :], kc == 0, kc == 7, [B_wD, BXN], [pb])
                        G, BG = gsb[gi % 3], B_gsb[gi % 3]
                        gi += 1
                        P.act(G[:], ps[:], AF.Sigmoid, [B_gc], [pb, BG], bias=gc[:, 16 + i * 8 + m:17 + i * 8 + m])
                        ps2, pb2 = rot.next()
                        for kc in range(4):
                            P.mm(ps2[:], wbr_sb[:, i * 4 + kc, m * 128:(m + 1) * 128], yz[i][:, kc, :], kc == 0, kc == 3,
                                 [B_wD, B_yz[i]], [pb2])
                        if i == 0:
                            P.tt("dve", AC[:], ps2[:], G[:], ALU.mult, [BG], [pb2, BAC])
                        else:
                            T_, BT = tmp[ti % 2], B_tmp[ti % 2]
                            ti += 1
                            P.tt("dve", T_[:], ps2[:], G[:], ALU.mult, [BG], [pb2, BT])
                            if i == 1:
                                P.tt("pool", AC[:], AC[:], T_[:], ALU.add, [BT], [BAC])
                            else:
                                P.tt("pool", MGt[:, m, :], AC[:], T_[:], ALU.add, [BT, BAC], [BMG])
                P.dma("pool", MG[:, :, t0:t0 + NT], MGt[:], r=[BMG])
        carryD.close()
        P.barrier()

        if not last:
            carryA = ExitStack()
            wA_sb = P.sb(carryA, "wA_sb", [128, 8, W_A], BF16)
            B_wA = Buf()
        with ExitStack() as st:
            wo_sb = P.sb(st, "wo_sb", [128, 8, D], BF16)
            B_wo = Buf()
            stage = [P.sb(st, f"stageE{i}", [128, 1024], F32) for i in range(2)]
            stageB = [Buf(), Buf()]
            gc = P.sb(st, "gcE", [128, 43], F32)
            B_gc = Buf()
            mgt = [P.sb(st, f"mgE{i}", [128, 8, NT], BF16) for i in range(2)]
            B_mgt = [Buf(), Buf()]
            xt = [P.sb(st, f"xtE{i}", [128, 8, NT], F32) for i in range(2)]
            B_xt = [Buf(), Buf()]
            o_sb = P.sb(st, "o_sb", [128, 8, NT], F32)
            B_o = Buf()
            sq = P.sb(st, "sqE", [128, 8, NT], BF16)
            B_sq = Buf()
            rstd = P.sb(st, "rstdE", [128, NT], F32)
            B_rstd = Buf()
            tmp = [P.sb(st, f"tmpE{i}", [128, NT], F32) for i in range(3)]
            B_tmp = [Buf(), Buf(), Buf()]
            xo = [P.sb(st, f"xoE{i}", [128, 8, NT], F32) for i in range(2)]
            B_xo = [Buf(), Buf()]
            k = [0]
            for c in range(8):
                for ch in load_chunks(wo_sb[:, c, :], B_wo, wout[l, c * 128:(c + 1) * 128, :], D, stage, stageB, k, 1024):
                    ch()
            achunks = []
            apend = []
            if not last:
                for c in range(8):
                    achunks += load_chunks(wA_sb[:, c, :], B_wA, wA[l + 1, c * 128:(c + 1) * 128, :], W_A, stage, stageB, k, 1024, "pool", apend)
            P.dma("sp", gc[:], gcols[l], w=[B_gc])
            rot = Rot([(psum[i], psB[i]) for i in range(1, 8)])
            o_sb2 = [o_sb, P.sb(st, "o_sbb", [128, 8, NT], F32)]
            B_o2 = [B_o, Buf()]
            sq2 = [sq, P.sb(st, "sqEb", [128, 8, NT], BF16)]
            B_sq2 = [B_sq, Buf()]

            def outproj(t):
                t0 = t * NT
                M_, BM = mgt[t % 2], B_mgt[t % 2]
                X, BX = xt[t % 2], B_xt[t % 2]
                O_, BO = o_sb2[t % 2], B_o2[t % 2]
                SQ, BSQ = sq2[t % 2], B_sq2[t % 2]
                P.dma("sp", M_[:], MG[:, :, t0:t0 + NT], w=[BM])
                P.dma("sp", X[:], x_src[:, :, t0:t0 + NT], w=[BX])
                for m in range(8):
                    ps, pb = rot.next()
                    for kc in range(8):
                        P.mm(ps[:], wo_sb[:, kc, m * 128:(m + 1) * 128], M_[:, kc, :], kc == 0, kc == 7, [B_wo, BM], [pb])
                    P.copy("dve", O_[:, m, :], ps[:], [], [pb, BO])
                    P.act(SQ[:, m, :], ps[:], AF.Square, [], [pb, BSQ])

            def post(t):
                t0 = t * NT
                X, BX = xt[t % 2], B_xt[t % 2]
                XO, BXO = xo[t % 2], B_xo[t % 2]
                O_, BO = o_sb2[t % 2], B_o2[t % 2]
                SQ, BSQ = sq2[t % 2], B_sq2[t % 2]
                for m in range(8):
                    P.mm(psum[0][:], ones_bf[:], SQ[:, m, :], m == 0, m == 7, [B_ones, BSQ], [psB[0]])
                P.act(rstd[:], psum[0][:], AF.Ln, [], [psB[0], B_rstd], bias=EPS, scale=1.0 / D)
                P.act(rstd[:], rstd[:], AF.Exp, [], [B_rstd], scale=-0.5)
                for m in range(8):
                    T_, BT = tmp[m % 3], B_tmp[m % 3]
                    P.stt("dve", T_[:], O_[:, m, :], gc[:, 8 + m:9 + m], rstd[:], ALU.mult, ALU.mult, [BO, B_gc, B_rstd], [BT])
                    P.tt("pool", XO[:, m, :], X[:, m, :], T_[:], ALU.add, [BX, BT], [BXO])
                d_ = P.dma("pool", x_dst[:, :, t0:t0 + NT], XO[:], r=[BXO])
                if last:
                    final_ops.append(d_)

            outproj(0)
            per_t = (len(achunks) + NTILE - 1) // NTILE
            for t in range(NTILE):
                if t + 1 < NTILE:
                    outproj(t + 1)
                for _ in range(per_t):
                    if achunks:
                        achunks.pop(0)()
                post(t)
            while achunks:
                achunks.pop(0)()
            while apend:
                apend.pop(0)()
        P.barrier()

    P.finish(final_ops)
    return nc


def prep_weights(inp, L, S):
    f32 = np.float32
    w_in = np.asarray(inp["w_in"], f32)[:L]
    sl = lambda a, b: w_in[:, :, a:b]
    kr = sl(1920, 1952)
    krx = np.concatenate([kr, kr[:, :, 16:32], kr[:, :, 0:16]], axis=2)
    wA = np.ascontiguousarray(np.concatenate(
        [sl(0, 512), sl(512, 1024), sl(1536, 1792), sl(1792, 1920), krx, sl(2464, 2976), sl(2976, 3488), sl(3488, 4000)], axis=2))
    wD = np.ascontiguousarray(np.concatenate([sl(1024, 1536), sl(1952, 2464), sl(4000, 4512), sl(4512, 7584)], axis=2))
    assert wA.shape[2] == W_A and wD.shape[2] == W_D
    uq = np.asarray(inp["mla_w_uq"], f32)[:L].reshape(L, 256, 8, 96)
    wuq = np.ascontiguousarray(np.concatenate([uq[..., 64:96], uq[..., 80:96], uq[..., 64:80], uq[..., 0:64]], axis=3).reshape(L, 256, 1024))
    wukv = np.ascontiguousarray(np.asarray(inp["mla_w_ukv"], f32)[:L])
    wbr = np.ascontiguousarray(np.asarray(inp["w_branch"], f32)[:L])
    wout = np.ascontiguousarray(np.asarray(inp["w_out"], f32)[:L])
    gcols = np.zeros((L, 128, 43), f32)
    gcols[:, :, 0:8] = np.asarray(inp["pre_g"], f32)[:L].reshape(L, 8, 128).transpose(0, 2, 1)
    gcols[:, :, 8:16] = np.asarray(inp["post_g"], f32)[:L].reshape(L, 8, 128).transpose(0, 2, 1)
    gcols[:, :, 16:40] = np.asarray(inp["gate_b"], f32)[:L].reshape(L, 3, 8, 128).transpose(0, 3, 1, 2).reshape(L, 128, 24)
    gcols[:, :, 40:42] = np.asarray(inp["mla_q_norm_g"], f32)[:L].reshape(L, 2, 128).transpose(0, 2, 1)
    gcols[:, :, 42] = np.asarray(inp["mla_kv_norm_g"], f32)[:L]
    lnrow = np.ascontiguousarray(np.stack([np.asarray(inp["sgu_ln_g"], f32)[:L], np.asarray(inp["sgu_ln_b"], f32)[:L]], axis=1))
    sb_ = np.asarray(inp["sgu_b"], f32)[:L]
    bsb = np.ascontiguousarray(np.repeat(sb_.reshape(L, 4, 2, 1, 128), 64, axis=3).transpose(0, 2, 3, 1, 4).reshape(L, 128, 4, 128))
    wsT = np.ascontiguousarray(np.asarray(inp["sgu_w"], f32)[:L].transpose(0, 3, 1, 2))
    rel = np.asarray(inp["ca_rel_bias"], f32)[:L]
    ki = np.arange(128)[:, None]
    cc = np.arange(RELW)[None, :]
    idx = np.clip(cc - 384 - ki, -128, 128) + 128
    relM = np.ascontiguousarray(rel[:, :, idx].transpose(0, 2, 1, 3))
    tri = (np.arange(128)[:, None] <= np.arange(128)[None, :]).astype(f32)
    half = 16
    inv = (np.float32(10000.0) ** (-np.arange(half, dtype=f32) / half)).astype(f32)
    ang = (np.arange(S, dtype=f32)[:, None] * inv[None, :]).astype(f32)
    cos = np.cos(ang).astype(f32).T
    sin = np.sin(ang).astype(f32).T
    c2 = np.concatenate([cos, cos], 0)
    s2 = np.concatenate([-sin, sin], 0)
    cs4 = np.ascontiguousarray(np.concatenate([c2, s2], 0))
    return dict(wA=wA, wD=wD, wuq=wuq, wukv=wukv, wbr=wbr, wout=wout, gcols=gcols, lnrow=lnrow, bsb=bsb, wsT=wsT,
                relM=relM, cs4=cs4, tri=tri)


_CORE_OF_BATCH = [0, 1, 4, 5]


def kernel(**inputs):
    x = np.asarray(inputs["x"], np.float32)
    B, S, _ = x.shape
    L = DEPTH
    w = prep_weights(inputs, L, S)
    nc = build_program(S, L)
    in_maps = []
    core_batch = {c: b for b, c in enumerate(_CORE_OF_BATCH)}
    zeros_x = np.zeros((D, S), np.float32)
    for core in range(8):
        m = dict(w)
        m["xT"] = np.ascontiguousarray(x[core_batch[core]].T) if core in core_batch else zeros_x
        in_maps.append(m)
    res = run_bass_kernel_spmd(nc, in_maps, core_ids=list(range(8)))
    out = np.empty((B, S, D), np.float32)
    for b, c in enumerate(_CORE_OF_BATCH):
        out[b] = res.results[c]["yout"].T
    return out
```

```python
import numpy as np
from contextlib import ExitStack
import concourse.bass as bass
import concourse.mybir as mybir
from concourse.bass_utils import run_bass_kernel_spmd

F32 = mybir.dt.float32
BF16 = mybir.dt.bfloat16
AF = mybir.ActivationFunctionType
ALU = mybir.AluOpType

D = 1024
DEPTH = 4
BATCH = 4
SEQ = 8192
EPS = 1e-6
NT = 512
W_A = 3008
W_D = 4608
RELW = 1152


class Buf:
    __slots__ = ("name", "last_w", "readers")

    def __init__(self, name=""):
        self.name = name
        self.last_w = None
        self.readers = []


class Op:
    __slots__ = ("eng", "fn", "deps", "dma", "sig", "cnt", "slot", "slotv")

    def __init__(self, eng, fn, dma):
        self.eng = eng
        self.fn = fn
        self.dma = dma
        self.deps = []
        self.sig = False
        self.cnt = 0
        self.slot = 0
        self.slotv = 0


ENGS = ("pe", "act", "dve", "pool", "sp")
STRICT_SAME_ENGINE = True
EPOCH = 30000
DMA_RING = 8


class Prog:
    def __init__(self, nc):
        self.nc = nc
        self.ops = []
        self.stack = ExitStack()
        self.last = {e: None for e in ENGS}
        self.dmas_since_bar = []

    def sb(self, stack, name, shape, dt):
        self.nsb = getattr(self, "nsb", 0) + 1
        return stack.enter_context(self.nc.sbuf_tensor(f"{name}_{self.nsb}", shape, dt))

    def op(self, eng, fn, reads=(), writes=(), dma=False):
        o = Op(eng, fn, dma)
        deps = {}
        for b in reads:
            if b.last_w is not None:
                deps[id(b.last_w)] = b.last_w
        for b in writes:
            if b.last_w is not None:
                deps[id(b.last_w)] = b.last_w
            for r in b.readers:
                deps[id(r)] = r
        for b in reads:
            b.readers.append(o)
        for b in writes:
            b.last_w = o
            b.readers = []
        for d in deps.values():
            if d is o:
                continue
            if (not d.dma) and d.eng == eng and (eng == "pe" or not STRICT_SAME_ENGINE):
                continue
            o.deps.append(d)
            d.sig = True
        self.ops.append(o)
        if dma:
            self.dmas_since_bar.append(o)
        else:
            self.last[eng] = o
        return o

    def barrier(self):
        deps = [o for o in self.last.values() if o is not None] + list(self.dmas_since_bar)
        for d in deps:
            d.sig = True
        for e in ENGS:
            o = Op(e, None, False)
            o.deps = list(deps)
            self.ops.append(o)
        self.dmas_since_bar = []

    def dma(self, eng, out, in_, r=(), w=()):
        return self.op(eng, lambda e: e.dma_start(out=out, in_=in_), r, w, dma=True)

    def mm(self, out, lhsT, rhs, start, stop, r, w):
        return self.op("pe", lambda e: e.matmul(out, lhsT, rhs, start=start, stop=stop), r, w)

    def act(self, out, in_, func, r, w, bias=None, scale=None):
        kw = {}
        if bias is not None:
            kw["bias"] = bias
        if scale is not None:
            kw["scale"] = scale
        return self.op("act", lambda e: e.activation(out=out, in_=in_, func=func, **kw), r, w)

    def copy(self, eng, out, in_, r, w):
        if eng == "act":
            return self.act(out, in_, AF.Copy, r, w)
        return self.op(eng, lambda e: e.tensor_copy(out=out, in_=in_), r, w)

    def tt(self, eng, out, in0, in1, op, r, w):
        return self.op(eng, lambda e: e.tensor_tensor(out=out, in0=in0, in1=in1, op=op), r, w)

    def ts(self, eng, out, in0, s1, s2, op0, op1, r, w):
        if op1 is None:
            return self.op(eng, lambda e: e.tensor_scalar(out=out, in0=in0, scalar1=s1, scalar2=None, op0=op0), r, w)
        return self.op(eng, lambda e: e.tensor_scalar(out=out, in0=in0, scalar1=s1, scalar2=s2, op0=op0, op1=op1), r, w)

    def stt(self, eng, out, in0, scalar, in1, op0, op1, r, w):
        return self.op(eng, lambda e: e.scalar_tensor_tensor(out=out, in0=in0, scalar=scalar, in1=in1, op0=op0, op1=op1), r, w)

    def recip(self, out, in_, r, w):
        return self.op("dve", lambda e: e.reciprocal(out=out, in_=in_), r, w)

    def bn_stats(self, out, in_, r, w):
        return self.op("dve", lambda e: e.bn_stats(out=out, in_=in_), r, w)

    def bn_aggr(self, out, in_, r, w):
        return self.op("dve", lambda e: e.bn_aggr(out=out, in_=in_), r, w)

    def memset(self, eng, ap, val, w):
        return self.op(eng, lambda e: e.memset(ap, val), (), w)

    def finish(self, final_ops):
        nc = self.nc
        for o in final_ops:
            o.sig = True
        cnt = {e: 0 for e in ENGS}
        dcnt = {e: 0 for e in ENGS}
        for o in self.ops:
            if o.dma:
                m = dcnt[o.eng]
                dcnt[o.eng] += 1
                o.slot = m % DMA_RING
                o.slotv = m // DMA_RING
            elif o.sig:
                cnt[o.eng] += 1
                o.cnt = cnt[o.eng]
        st = self.stack
        sems = {e: [st.enter_context(nc.semaphore(f"s_{e}{i}")) for i in range(cnt[e] // EPOCH + 1)] for e in ENGS}
        rings = {e: ([st.enter_context(nc.semaphore(f"d_{e}{i}")) for i in range(DMA_RING)] if dcnt[e] else [])
                 for e in ENGS}
        ops = self.ops

        def emit(engname, e):
            known = {}

            def wait(key, sem, val):
                if known.get(key, 0) >= val:
                    return
                known[key] = val
                e.wait_ge(sem, val)

            for o in ops:
                if o.eng != engname:
                    continue
                for d in o.deps:
                    if d.dma:
                        wait(("d", d.eng, d.slot), rings[d.eng][d.slot], 16 * (d.slotv + 1))
                    else:
                        ep = (d.cnt - 1) // EPOCH
                        wait(("c", d.eng, ep), sems[d.eng][ep], (d.cnt - 1) % EPOCH + 1)
                if o.fn is None:
                    continue
                if o.dma:
                    if o.slotv > 0:
                        wait(("d", o.eng, o.slot), rings[o.eng][o.slot], 16 * o.slotv)
                    o.fn(e).then_inc(rings[o.eng][o.slot], 16)
                else:
                    ins = o.fn(e)
                    if o.sig:
                        ins.then_inc(sems[o.eng][(o.cnt - 1) // EPOCH], 1)
            if engname == "sp":
                for d in final_ops:
                    wait(("d", d.eng, d.slot), rings[d.eng][d.slot], 16 * (d.slotv + 1))

        with nc.Block() as block:
            @block.tensor
            def _(e):
                emit("pe", e)

            @block.scalar
            def _(e):
                emit("act", e)

            @block.vector
            def _(e):
                emit("dve", e)

            @block.gpsimd
            def _(e):
                emit("pool", e)

            @block.sync
            def _(e):
                emit("sp", e)
        self.stack.close()


class Rot:
    def __init__(self, items):
        self.items = items
        self.i = 0

    def next(self):
        it = self.items[self.i % len(self.items)]
        self.i += 1
        return it


def build_program(S, L, dbg=False):
    nc = bass.Bass("TRN2", target_bir_lowering=False)
    NTILE = S // NT
    P = Prog(nc)

    def din(name, shape, dt=F32):
        return nc.dram_tensor(name, shape, dt, kind="ExternalInput").ap()

    def dscr(name, shape, dt):
        return nc.dram_tensor(name, shape, dt, kind=("ExternalOutput" if dbg else "Internal")).ap()

    xT = din("xT", [D, S])
    wA = din("wA", [L, D, W_A])
    wD = din("wD", [L, D, W_D])
    wuq = din("wuq", [L, 256, 1024])
    wukv = din("wukv", [L, 128, 1024])
    wbr = din("wbr", [L, 3, 512, D])
    wout = din("wout", [L, D, D])
    gcols = din("gcols", [L, 128, 43])
    lnrow = din("lnrow", [L, 2, 512])
    bsb = din("bsb", [L, 128, 4, 128])
    wsT = din("wsT", [L, 128, 8, 128])
    relM = din("relM", [L, 128, 8, RELW])
    cs4 = din("cs4", [64, S])
    tri = din("tri", [128, 128])
    yout = nc.dram_tensor("yout", [D, S], F32, kind="ExternalOutput").ap()

    XN = dscr("XN", [128, 8, S], BF16)
    XR = dscr("XR", [128, 8, S], F32)
    YA = dscr("YA", [128, 4, S], BF16)
    YB = dscr("YB", [128, 4, S], BF16)
    YC = dscr("YC", [128, 4, S], BF16)
    CQ = dscr("CQ", [128, 2, S], BF16)
    CKV = dscr("CKV", [128, S], BF16)
    KR = dscr("KR", [32, S], BF16)
    QC = dscr("QC", [128, 4, S], BF16)
    KC = dscr("KC", [128, 4, S], BF16)
    VC = dscr("VC", [S, 1024], BF16)
    MG = dscr("MG", [128, 8, S], BF16)
    DBG = dscr("DBG", [128, 2, S], F32) if dbg else None

    xT_v = xT.rearrange("(c p) s -> p c s", p=128)
    yout_v = yout.rearrange("(c p) s -> p c s", p=128)
    VC_v = VC.rearrange("(n p) f -> p n f", p=128)

    psum = [P.stack.enter_context(nc.psum_tensor(f"ps{i}", [128, 512], F32)) for i in range(8)]
    psB = [Buf(f"ps{i}") for i in range(8)]

    ones_bf = P.sb(P.stack, "ones_bf", [128, 128], BF16)
    B_ones = Buf("ones")
    P.memset("pool", ones_bf[:], 1.0, [B_ones])

    final_ops = []

    def load_chunks(dst, dstB, src_rows, ncols, stage, stageB, k, piece, cast_eng=None, pend=None):
        out = []
        off = 0
        while off < ncols:
            n = min(piece, ncols - off)

            def f(off=off, n=n):
                sl = k[0] % 2
                k[0] += 1
                if pend is None:
                    P.dma("sp", stage[sl][:, 0:n], src_rows[:, off:off + n], w=[stageB[sl]])
                    eng = cast_eng if cast_eng else ("dve" if k[0] % 2 else "act")
                    P.copy(eng, dst[:, off:off + n], stage[sl][:, 0:n], [stageB[sl]], [dstB])
                else:
                    P.dma("pool", stage[sl][:, 0:n], src_rows[:, off:off + n], w=[stageB[sl]])
                    while pend:
                        pend.pop(0)()
                    pend.append(lambda: P.copy(cast_eng, dst[:, off:off + n], stage[sl][:, 0:n], [stageB[sl]], [dstB]))
            out.append(f)
            off += n
        return out

    carryA = ExitStack()
    wA_sb = P.sb(carryA, "wA_sb", [128, 8, W_A], BF16)
    B_wA = Buf()
    with ExitStack() as tmpst:
        stage0 = [P.sb(tmpst, f"stage0{i}", [128, 1504], F32) for i in range(2)]
        stage0B = [Buf(), Buf()]
        k0 = [0]
        for c in range(8):
            for ch in load_chunks(wA_sb[:, c, :], B_wA, wA[0, c * 128:(c + 1) * 128, :], W_A, stage0, stage0B, k0, 1504):
                ch()
    P.barrier()

    for l in range(L):
        last = (l == L - 1)
        x_src = xT_v if l == 0 else XR
        x_dst = yout_v if last else XR

        with ExitStack() as st:
            gc = P.sb(st, "gcA", [128, 43], F32)
            B_gc = Buf()
            lng = P.sb(st, "lng", [128, 512], F32)
            lnb = P.sb(st, "lnb", [128, 512], F32)
            B_ln = Buf()
            bsb_sb = P.sb(st, "bsb_sb", [128, 4, 128], F32)
            B_bsb = Buf()
            ws_f = P.sb(st, "ws_f", [128, 8, 128], F32)
            ws_sb = P.sb(st, "ws_sb", [128, 8, 128], BF16)
            tri_sb = P.sb(st, "tri_sb", [128, 128], F32)
            B_ws = Buf()
            xt = [P.sb(st, f"xt{i}", [128, 8, NT], F32) for i in range(2)]
            B_xt = [Buf(), Buf()]
            sq = P.sb(st, "sqA", [128, 8, NT], BF16)
            B_sq = Buf()
            rstd = P.sb(st, "rstdA", [128, NT], F32)
            B_rstd = Buf()
            xn = [P.sb(st, f"xnA{i}", [128, 8, NT], BF16) for i in range(2)]
            B_xn = [Buf(), Buf()]
            uA2 = [P.sb(st, f"uA{i}", [128, 4, NT], BF16) for i in range(2)]
            B_uA2 = [Buf(), Buf()]
            qd_sb = P.sb(st, "qd_sb", [128, 3, NT], F32)
            qsq = P.sb(st, "qsq", [128, 3, NT], BF16)
            B_qd = Buf()
            B_qsq = Buf()
            rq = P.sb(st, "rq", [128, 2, NT], F32)
            B_rq = Buf()
            rq2 = P.sb(st, "rq2", [128, 2, NT], F32)
            B_rq2 = Buf()
            cq_sb = P.sb(st, "cq_sbA", [128, 3, NT], BF16)
            B_cq = Buf()
            cst = P.sb(st, "cstA", [64, NT], F32)
            B_cst = Buf()
            krt = P.sb(st, "krt", [64, NT], F32)
            kr_bf = P.sb(st, "kr_bf", [32, NT], BF16)
            krs = P.sb(st, "krs", [32, NT], F32)
            B_kr = Buf()
            qc_sb = P.sb(st, "qc_sbA", [128, 4, NT], BF16)
            kc_sb = P.sb(st, "kc_sbA", [128, 4, NT], BF16)
            B_qc = Buf()
            B_kc = Buf()
            vaug = [P.sb(st, f"vaugA{i}", [128, 4, 1024], BF16) for i in range(2)]
            B_vaug = [Buf(), Buf()]
            vn0 = [P.sb(st, f"vn0_{i}", [128, NT], F32) for i in range(2)]
            vn1 = [P.sb(st, f"vn1_{i}", [128, NT], F32) for i in range(2)]
            B_vn1 = [Buf(), Buf()]
            vn = [P.sb(st, f"vn{i}", [128, NT], BF16) for i in range(2)]
            B_vn0 = [Buf(), Buf()]
            B_vn = [Buf(), Buf()]
            stt6 = P.sb(st, "bnst", [128, 4, 6], F32)
            mv = P.sb(st, "bnmv", [128, 4, 4], F32)
            B_mv = Buf()
            ytmp2 = [P.sb(st, f"ytmp{i}", [128, 4, 128], F32) for i in range(2)]
            B_ytmp2 = [Buf(), Buf()]
            yA_sb = P.sb(st, "yA_sb", [128, 4, NT], BF16)
            B_yA = Buf()

            P.dma("sp", gc[:], gcols[l], w=[B_gc])
            P.dma("sp", lng[:], lnrow[l, 0:1, :].partition_broadcast(128), w=[B_ln])
            P.dma("sp", lnb[:], lnrow[l, 1:2, :].partition_broadcast(128), w=[B_ln])
            P.dma("sp", bsb_sb[:], bsb[l], w=[B_bsb])
            P.dma("sp", ws_f[:], wsT[l], w=[B_ws])
            P.dma("sp", tri_sb[:], tri[:, :], w=[B_ws])
            for g in range(8):
                P.tt("dve", ws_sb[:, g, :], ws_f[:, g, :], tri_sb[:], ALU.mult, [], [B_ws])
            for i in range(2):
                va = vaug[i][:].rearrange("p j (hp two e) -> p j hp two e", two=2, e=128)
                for j in range(4):
                    P.memset("pool", va[:, j, :, 0, 64:128], 1.0, [B_vaug[i]])
                    P.memset("pool", va[:, j, :, 1, 0:64], 1.0, [B_vaug[i]])

            rot = Rot([(psum[i], psB[i]) for i in range(1, 8)])
            vnb = [P.sb(st, f"vnb{i}", [128, NT], BF16) for i in range(2)]
            B_vnb = [Buf(), Buf()]
            vraw = [P.sb(st, f"vraw{i}", [128, NT], F32) for i in range(4)]
            B_vraw = [Buf() for _ in range(4)]
            vn4 = [vn[0], vn[1], vnb[0], vnb[1]]
            B_vn4 = [B_vn[0], B_vn[1], B_vnb[0], B_vnb[1]]

            def normA(t):
                t0 = t * NT
                X, BX = xt[t % 2], B_xt[t % 2]
                XNt, BXN = xn[t % 2], B_xn[t % 2]
                P.dma("sp", X[:], x_src[:, :, t0:t0 + NT], w=[BX])
                for c in range(8):
                    P.act(sq[:, c, :], X[:, c, :], AF.Square, [BX], [B_sq])
                for c in range(8):
                    P.mm(psum[0][:], ones_bf[:], sq[:, c, :], c == 0, c == 7, [B_ones, B_sq], [psB[0]])
                P.act(rstd[:], psum[0][:], AF.Ln, [], [psB[0], B_rstd], bias=EPS, scale=1.0 / D)
                P.act(rstd[:], rstd[:], AF.Exp, [], [B_rstd], scale=-0.5)
                for c in range(8):
                    P.stt("dve", XNt[:, c, :], X[:, c, :], gc[:, c:c + 1], rstd[:],
                          ALU.mult, ALU.mult, [BX, B_gc, B_rstd], [BXN])
                P.dma("pool", XN[:, :, t0:t0 + NT], XNt[:], r=[BXN])

            def projA(t):
                t0 = t * NT
                XNt, BXN = xn[t % 2], B_xn[t % 2]
                uA, B_uA = uA2[t % 2], B_uA2[t % 2]
                P.dma("sp", cst[:], cs4[0:64, t0:t0 + NT], w=[B_cst])

                def fm(off, M):
                    ps, pb = rot.next()
                    for c in range(8):
                        P.mm(ps[0:M, :], wA_sb[:, c, off:off + M], XNt[:, c, :], c == 0, c == 7, [B_wA, BXN], [pb])
                    return ps, pb

                vps = []
                for j in range(4):
                    ps, pb = rot.next()
                    for c in range(8):
                        P.mm(ps[:], XNt[:, c, j * 128:(j + 1) * 128], wA_sb[:, c, 512:1024], c == 0, c == 7,
                             [B_wA, BXN], [pb])
                    P.copy("act", vraw[j][:], ps[:], [], [pb, B_vraw[j]])
                for j in range(4):
                    ps, pb = fm(j * 128, 128)
                    P.copy("act", uA[:, j, :], ps[:], [], [pb, B_uA])
                for j in range(4):
                    P.bn_stats(stt6[:, j, :], vraw[j][:], [B_vraw[j]], [B_mv])
                    P.bn_aggr(mv[:, j, 0:2], stt6[:, j, :], [], [B_mv])
                for c3 in range(3):
                    ps, pb = fm(1024 + c3 * 128, 128)
                    P.copy("act", qd_sb[:, c3, :], ps[:], [], [pb, B_qd])
                    P.act(qsq[:, c3, :], ps[:], AF.Square, [], [pb, B_qsq])
                P.act(mv[:, :, 2], mv[:, :, 1], AF.Ln, [], [B_mv], bias=EPS, scale=1.0)
                ps, pb = fm(1408, 64)
                P.tt("dve", krt[32:64, :], ps[32:64, :], cst[32:64, :], ALU.mult, [B_cst], [pb, B_kr])
                P.tt("dve", krt[0:32, :], ps[0:32, :], cst[0:32, :], ALU.mult, [B_cst], [pb, B_kr])
                P.act(mv[:, :, 3], mv[:, :, 2], AF.Exp, [], [B_mv], scale=-0.5)
                P.stt("dve", mv[:, :, 2], mv[:, :, 0], -1.0, mv[:, :, 3], ALU.mult, ALU.mult, [], [B_mv])
                for cg in range(4):
                    ps, pb = fm(1472 + cg * 128, 128)
                    P.copy("act", qc_sb[:, cg, :], ps[:], [], [pb, B_qc])
                P.dma("pool", QC[:, :, t0:t0 + NT], qc_sb[:], r=[B_qc])
                P.copy("act", krs[0:32, :], krt[32:64, :], [], [B_kr])
                for j in range(4):
                    P.act(vn0[j % 2][:], vraw[j][:], AF.Identity, [B_mv, B_vraw[j]], [B_vn0[j % 2]], bias=mv[:, j, 2:3], scale=mv[:, j, 3:4])
                    P.tt("dve", vn1[j % 2][:], vn0[j % 2][:], lng[:], ALU.mult, [B_ln, B_vn0[j % 2]], [B_vn1[j % 2]])
                    P.tt("pool", vn4[j][:], vn1[j % 2][:], lnb[:], ALU.add, [B_ln, B_vn1[j % 2]], [B_vn4[j]])
                P.tt("pool", kr_bf[:], krt[0:32, :], krs[0:32, :], ALU.add, [], [B_kr])
                P.dma("pool", KR[:, t0:t0 + NT], kr_bf[:], r=[B_kr])
                for cg in range(4):
                    ps, pb = fm(1984 + cg * 128, 128)
                    P.copy("dve" if cg % 2 else "act", kc_sb[:, cg, :], ps[:], [], [pb, B_kc])
                P.dma("pool", KC[:, :, t0:t0 + NT], kc_sb[:], r=[B_kc])
                ps, pb = rot.next()
                P.mm(ps[:], ones_bf[:], qsq[:, 0, :], True, False, [B_ones, B_qsq], [pb])
                P.mm(ps[:], ones_bf[:], qsq[:, 1, :], False, True, [B_ones, B_qsq], [pb])
                P.act(rq[:, 0, :], ps[:], AF.Ln, [], [pb, B_rq], bias=EPS, scale=1.0 / 256)
                ps, pb = rot.next()
                P.mm(ps[:], ones_bf[:], qsq[:, 2, :], True, True, [B_ones, B_qsq], [pb])
                P.act(rq[:, 1, :], ps[:], AF.Ln, [], [pb, B_rq], bias=EPS, scale=1.0 / 128)
                VA_ = vaug[t % 2]
                BVA = B_vaug[t % 2]
                va = VA_[:].rearrange("p j (hp two e) -> p j hp two e", two=2, e=128)
                for j in range(4):
                    ps, pb = rot.next()
                    for c in range(8):
                        P.mm(ps[:], XNt[:, c, j * 128:(j + 1) * 128], wA_sb[:, c, 2496:3008], c == 0, c == 7,
                             [B_wA, BXN], [pb])
                    pv = ps[:].rearrange("p (hp two d) -> p hp two d", two=2, d=64)
                    P.copy("act", va[:, j, :, 0, 0:64], pv[:, :, 0, :], [], [pb, BVA])
                    P.copy("dve", va[:, j, :, 1, 64:128], pv[:, :, 1, :], [], [pb, BVA])
                P.dma("pool", VC_v[:, 4 * t:4 * t + 4, :], VA_[:], r=[BVA])
                P.act(rq2[:], rq[:], AF.Exp, [B_rq], [B_rq2], scale=-0.5)
                for c3 in range(3):
                    P.stt("dve", cq_sb[:, c3, :], qd_sb[:, c3, :], gc[:, 40 + c3:41 + c3],
                          rq2[:, 0 if c3 < 2 else 1, :], ALU.mult, ALU.mult, [B_qd, B_gc, B_rq2], [B_cq])
                P.dma("pool", CQ[:, :, t0:t0 + NT], cq_sb[:, 0:2, :], r=[B_cq])
                P.dma("pool", CKV[:, t0:t0 + NT], cq_sb[:, 2, :], r=[B_cq])
                for j in range(4):
                    ps2, pb2 = rot.next()
                    for g in range(8):
                        hp = (g % 2) * 64
                        cg = g // 2
                        P.mm(ps2[hp:hp + 64, cg * 128:(cg + 1) * 128], vn4[j][:, g * 64:(g + 1) * 64], ws_sb[:, g, :],
                             True, True, [B_vn4[j], B_ws], [pb2])
                    ytmp, B_ytmp = ytmp2[j % 2], B_ytmp2[j % 2]
                    P.tt("dve", ytmp[:], ps2[:].rearrange("p (c t) -> p c t", t=128), bsb_sb[:], ALU.add,
                         [B_bsb], [pb2, B_ytmp])
                    P.tt("pool", yA_sb[:, :, j * 128:(j + 1) * 128], ytmp[:], uA[:, :, j * 128:(j + 1) * 128],
                         ALU.mult, [B_ytmp, B_uA], [B_yA])
                P.dma("pool", YA[:, :, t0:t0 + NT], yA_sb[:], r=[B_yA])

            normA(0)
            for t in range(NTILE):
                if t + 1 < NTILE:
                    normA(t + 1)
                projA(t)
        carryA.close()
        P.barrier()

        with ExitStack() as st:
            ckv_sb = P.sb(st, "ckv_sb", [128, S], BF16)
            cqr = P.sb(st, "cqr", [128, 2, S], BF16)
            B_ckv = Buf()
            B_cqr = Buf()
            kT = [P.sb(st, f"kT{i}", [128, S], BF16) for i in range(2)]
            B_kT = [Buf(), Buf()]
            qT = [P.sb(st, f"qT{i}", [128, S], BF16) for i in range(2)]
            B_qT = [Buf(), Buf()]
            vT = [P.sb(st, f"vT{i}", [128, S // 128, 128], BF16) for i in range(2)]
            B_vT = [Buf(), Buf()]
            yb_sb = [P.sb(st, f"yb_sb{i}", [128, S], BF16) for i in range(2)]
            B_yb = [Buf(), Buf()]
            wuq_sb = P.sb(st, "wuq_sb", [128, 2, 1024], BF16)
            wukv_sb = P.sb(st, "wukv_sb", [128, 1024], BF16)
            B_wC = Buf()
            stage = [P.sb(st, f"stageC{i}", [128, 512], F32) for i in range(2)]
            stageB = [Buf(), Buf()]
            cst = [P.sb(st, f"cstC{i}", [64, NT], F32) for i in range(2)]
            B_cst = [Buf(), Buf()]
            t12 = P.sb(st, "t12", [64, NT], F32)
            t3 = P.sb(st, "t3", [32, NT], F32)
            B_t12 = Buf()
            NPC = 6
            p_sb = [P.sb(st, f"p_sbC{i}", [128, NT], BF16) for i in range(NPC)]
            B_p = [Buf() for _ in range(NPC)]
            pd = [P.sb(st, f"pdC{i}", [128, NT], BF16) for i in range(4)]
            B_pd = [Buf() for _ in range(4)]
            for i in range(4):
                P.memset("pool", pd[i][:], 0.0, [B_pd[i]])
            rec = P.sb(st, "recC", [128, NT], F32)
            B_rec = Buf()
            k = [0]
            for kc in range(2):
                for ch in load_chunks(wuq_sb[:, kc, :], B_wC, wuq[l, kc * 128:(kc + 1) * 128, :], 1024, stage, stageB, k, 512):
                    ch()
            for ch in load_chunks(wukv_sb[:, :], B_wC, wukv[l, :, :], 1024, stage, stageB, k, 512):
                ch()
            P.dma("sp", ckv_sb[:], CKV[:, :], w=[B_ckv])
            for kc in range(2):
                P.dma("sp", cqr[:, kc, :], CQ[:, kc, :], w=[B_cqr])
            for i in range(2):
                P.dma("sp", kT[i][0:32, :], KR[:, :], w=[B_kT[i]])
                P.memset("pool", kT[i][32:64, :], 0.0, [B_kT[i]])
                P.memset("pool", qT[i][32:64, :], 0.0, [B_qT[i]])
            P.memset("pool", vT[0][:, :, 64:128], 1.0, [B_vT[0]])
            P.memset("pool", vT[1][:, :, 0:64], 1.0, [B_vT[1]])
            grot = Rot([(psum[i], psB[i]) for i in range(0, 2)])
            srot = Rot([(psum[i], psB[i]) for i in range(2, 6)])
            orot = Rot([(psum[i], psB[i]) for i in range(6, 8)])
            scale = 96 ** -0.5
            cctr = [0]

            def gen_chunks(h):
                sl = h % 2
                K_, BK = kT[sl], B_kT[sl]
                Q_, BQ = qT[sl], B_qT[sl]
                V_, BV = vT[sl], B_vT[sl]
                voff = 0 if sl == 0 else 64
                out = []

                def kchunk(tt_):
                    ps, pb = grot.next()
                    P.mm(ps[64:128, :], wukv_sb[:, h * 128:h * 128 + 64], ckv_sb[:, tt_ * NT:(tt_ + 1) * NT], True, True,
                         [B_wC, B_ckv], [pb])
                    P.copy("dve", K_[64:128, tt_ * NT:(tt_ + 1) * NT], ps[64:128, :], [], [pb, BK])

                def vchunk(t8):
                    ps, pb = grot.next()
                    for jj in range(8):
                        kt = t8 * 8 + jj
                        P.mm(ps[:, jj * 64:(jj + 1) * 64], ckv_sb[:, kt * 128:(kt + 1) * 128],
                             wukv_sb[:, h * 128 + 64:h * 128 + 128], True, True, [B_wC, B_ckv], [pb])
                    P.copy("dve", V_[:, t8 * 8:(t8 + 1) * 8, voff:voff + 64],
                           ps[:].rearrange("p (j d) -> p j d", d=64), [], [pb, BV])

                def qchunk(tt_):
                    CS, BCS = cst[cctr[0] % 2], B_cst[cctr[0] % 2]
                    cctr[0] += 1
                    P.dma("sp", CS[:], cs4[0:64, tt_ * NT:(tt_ + 1) * NT], w=[BCS])
                    ps, pb = grot.next()
                    for kc in range(2):
                        P.mm(ps[:], wuq_sb[:, kc, h * 128:(h + 1) * 128], cqr[:, kc, tt_ * NT:(tt_ + 1) * NT], kc == 0, kc == 1,
                             [B_wC, B_cqr], [pb])
                    P.tt("dve", t12[0:64, :], ps[0:64, :], CS[0:64, :], ALU.mult, [BCS], [pb, B_t12])
                    P.copy("dve", Q_[64:128, tt_ * NT:(tt_ + 1) * NT], ps[64:128, :], [], [pb, BQ])
                    P.copy("act", t3[0:32, :], t12[32:64, :], [], [B_t12])
                    P.tt("pool", Q_[0:32, tt_ * NT:(tt_ + 1) * NT], t12[0:32, :], t3[0:32, :], ALU.add, [], [B_t12, BQ])

                for tt_ in range(NTILE):
                    out.append(lambda tt_=tt_: kchunk(tt_))
                for t8 in range(S // 1024):
                    out.append(lambda t8=t8: vchunk(t8))
                for tt_ in range(NTILE):
                    out.append(lambda tt_=tt_: qchunk(tt_))
                return out

            for ch in gen_chunks(0):
                ch()
            LA = 5
            it = 0
            for h in range(8):
                sl = h % 2
                c = h // 2
                K_, BK = kT[sl], B_kT[sl]
                Q_, BQ = qT[sl], B_qT[sl]
                V_, BV = vT[sl], B_vT[sl]
                Y_, BY = yb_sb[c % 2], B_yb[c % 2]
                nxt = gen_chunks(h + 1) if h + 1 < 8 else []
                items = [(g, j) for g in range(NTILE) for j in range(4 * g + 4)]
                every = max(1, len(items) // (len(nxt) + 1)) if nxt else 0
                pend = []
                obank = {}

                def emit_pv(item):
                    g, j, c0, PT, BP = item
                    nj = 4 * g + 4
                    if j == 0:
                        obank[g] = orot.next()
                    ops_, opb = obank[g]
                    P.mm(ops_[:, c0:NT], V_[:, j, :], PT[:, c0:NT], j == 0, j == nj - 1, [BV, BP], [opb])
                    if j == nj - 1:
                        if sl == 0:
                            P.recip(rec[64:128, :], ops_[64:128, :], [], [opb, B_rec])
                            P.tt("dve", Y_[0:64, g * NT:(g + 1) * NT], ops_[0:64, :], rec[64:128, :], ALU.mult, [B_rec], [opb, BY])
                        else:
                            P.recip(rec[0:64, :], ops_[0:64, :], [], [opb, B_rec])
                            P.tt("dve", Y_[64:128, g * NT:(g + 1) * NT], ops_[64:128, :], rec[0:64, :], ALU.mult, [B_rec], [opb, BY])
                        del obank[g]

                for idx, (g, j) in enumerate(items):
                    r_ = j - 4 * g
                    c0 = 128 * r_ if r_ > 0 else 0
                    sps, spb = srot.next()
                    P.mm(sps[:, c0:NT], K_[:, j * 128:(j + 1) * 128], Q_[:, g * NT + c0:(g + 1) * NT], True, True,
                         [BK, BQ], [spb])
                    if r_ >= 0:
                        PT, BP = pd[r_], B_pd[r_]
                    else:
                        PT, BP = p_sb[it % NPC], B_p[it % NPC]
                        it += 1
                    P.act(PT[:, c0:NT], sps[:, c0:NT], AF.Exp, [], [spb, BP], scale=scale)
                    if r_ >= 0:
                        P.memset("pool", PT[64:128, c0:c0 + 64], 0.0, [BP])
                    pend.append((g, j, c0, PT, BP))
                    if len(pend) > LA:
                        emit_pv(pend.pop(0))
                    if nxt and every and (idx % every == every - 1):
                        nxt.pop(0)()
                while pend:
                    emit_pv(pend.pop(0))
                while nxt:
                    nxt.pop(0)()
                if sl == 1:
                    P.dma("pool", YB[:, c, :], Y_[:], r=[BY])
        P.barrier()

        carryD = ExitStack()
        wD_sb = P.sb(carryD, "wD_sb", [128, 8, W_D], BF16)
        wbr_sb = P.sb(carryD, "wbr_sb", [128, 12, D], BF16)
        B_wD = Buf()
        stageD = [P.sb(carryD, f"stageD{i}", [128, 1152], F32) for i in range(2)]
        stageDB = [Buf(), Buf()]
        kD = [0]
        dchunks = []
        dpend = []
        for c in range(8):
            dchunks += load_chunks(wD_sb[:, c, :], B_wD, wD[l, c * 128:(c + 1) * 128, :], W_D, stageD, stageDB, kD, 1152, "pool", dpend)
        for i in range(3):
            for kc in range(4):
                dchunks += load_chunks(wbr_sb[:, i * 4 + kc, :], B_wD, wbr[l, i, kc * 128:(kc + 1) * 128, :], D, stageD, stageDB, kD,
                                       1024, "pool", dpend)
        with ExitStack() as st:
            rel_sb = P.sb(st, "rel_sb", [128, 8, RELW], F32)
            B_rel = Buf()
            P.dma("sp", rel_sb[:, 0:4, :], relM[l, :, 0:4, :], w=[B_rel])
            P.dma("sp", rel_sb[:, 4:8, :], relM[l, :, 4:8, :], w=[B_rel])
            qct = [P.sb(st, f"qct{i}", [128, 4, NT], BF16) for i in range(2)]
            B_qct = [Buf(), Buf()]
            kct = [P.sb(st, f"kct{i}", [128, 4, NT], BF16) for i in range(3)]
            B_kct = [Buf(), Buf(), Buf()]
            vt = [P.sb(st, f"vtB{i}", [128, 4, 1024], BF16) for i in range(3)]
            B_vt = [Buf(), Buf(), Buf()]
            NSC, NPB = 6, 8
            sc_sb = [P.sb(st, f"sc_sb{i}", [128, NT], BF16) for i in range(NSC)]
            cbias = P.sb(st, "cbias", [128, 8], F32)
            B_cb = Buf()
            P.copy("dve", cbias[:], rel_sb[:, :, RELW - 1], [B_rel], [B_cb])
            for h_ in range(8):
                P.act(rel_sb[:, h_, :], rel_sb[:, h_, :], AF.Exp, [B_cb], [B_rel])
            B_sc = [Buf() for _ in range(NSC)]
            p_sb = [P.sb(st, f"p_sbB{i}", [128, NT], BF16) for i in range(NPB)]
            B_p = [Buf() for _ in range(NPB)]
            for i in range(NPB):
                P.memset("pool", p_sb[i][:], 0.0, [B_p[i]])
            rec2 = [P.sb(st, f"recB{i}", [128, NT], F32) for i in range(2)]
            B_rec2 = [Buf(), Buf()]
            yc_sb = [P.sb(st, "yc_sb0", [128, 4, NT], BF16)] * 2
            B_yc = [Buf()] * 2
            srot = Rot([(psum[i], psB[i]) for i in range(0, 5)])
            orot = Rot([(psum[i], psB[i]) for i in range(5, 8)])
            scale = 64 ** -0.5
            LA = 6
            pend = []
            obank = {}

            def emit_pv(item):
                g, h, j, js, src, sub, c0, c1, PT, BP = item
                c = h // 2
                Y, BY = yc_sb[g % 2], B_yc[g % 2]
                if j == js[0]:
                    obank[(g, h)] = orot.next()
                ops_, opb = obank[(g, h)]
                P.mm(ops_[:, c0:c1], vt[src][:, sub, h * 128:(h + 1) * 128], PT[:, c0:c1], j == js[0], j == js[-1],
                     [B_vt[src], BP], [opb])
                if j == js[-1]:
                    RC, BRC = rec2[h % 2], B_rec2[h % 2]
                    if h % 2 == 0:
                        P.act(RC[64:128, :], ops_[64:128, :], AF.Ln, [], [opb, BRC])
                        P.act(RC[64:128, :], RC[64:128, :], AF.Exp, [], [BRC], scale=-1.0)
                        P.tt("dve", Y[0:64, c, :], ops_[0:64, :], RC[64:128, :], ALU.mult, [BRC], [opb, BY])
                    else:
                        P.act(RC[0:64, :], ops_[0:64, :], AF.Ln, [], [opb, BRC])
                        P.act(RC[0:64, :], RC[0:64, :], AF.Exp, [], [BRC], scale=-1.0)
                        P.tt("dve", Y[64:128, c, :], ops_[64:128, :], RC[0:64, :], ALU.mult, [BRC], [opb, BY])
                    del obank[(g, h)]
                    if h == 7:
                        P.dma("pool", YC[:, :, g * NT:(g + 1) * NT], Y[:], r=[BY])

            it = 0
            n_items = 8 * (8 * NTILE - 4)
            every_d = max(1, n_items // (len(dchunks) + 1))
            for g in range(NTILE):
                t0 = g * NT
                Q = qct[g % 2]
                BQ = B_qct[g % 2]
                P.dma("sp", Q[:], QC[:, :, t0:t0 + NT], w=[BQ])
                P.dma("sp", kct[g % 3][:], KC[:, :, t0:t0 + NT], w=[B_kct[g % 3]])
                P.dma("sp", vt[g % 3][:], VC_v[:, 4 * g:4 * g + 4, :], w=[B_vt[g % 3]])
                for h in range(8):
                    c = h // 2
                    hp = (h % 2) * 64
                    js = [4, 5, 6, 7] if g == 0 else [3, 4, 0, 1, 2, 5, 6, 7]
                    for j in js:
                        src = (g - 1) % 3 if j < 4 else g % 3
                        sub = j % 4
                        c0 = 64 * max(0, 2 * j - 8)
                        c1 = 64 * (min(7, 2 * j + 1) + 1)
                        cstart = 640 if j <= 2 else 512 - 128 * (j - 3)
                        sps, spb = srot.next()
                        P.mm(sps[:, c0:c1], kct[src][hp:hp + 64, c, sub * 128:(sub + 1) * 128], Q[hp:hp + 64, c, c0:c1],
                             True, True, [B_kct[src], BQ], [spb])
                        SC, BSC = sc_sb[it % NSC], B_sc[it % NSC]
                        PT, BP = p_sb[j], B_p[j]
                        it += 1
                        if j <= 2:
                            P.act(PT[:, c0:c1], sps[:, c0:c1], AF.Exp, [B_cb], [spb, BP], bias=cbias[:, h:h + 1], scale=scale)
                            meng = "pool"
                        else:
                            P.act(SC[:, c0:c1], sps[:, c0:c1], AF.Exp, [], [spb, BSC], scale=scale)
                            meng = "dve" if it % 3 else "pool"
                            P.tt(meng, PT[:, c0:c1], SC[:, c0:c1], rel_sb[:, h, cstart + c0:cstart + c1], ALU.mult,
                                 [B_rel, BSC], [BP])
                        if j <= 3:
                            P.memset(meng, PT[0:64, c1 - 64:c1], 0.0, [BP])
                        else:
                            P.memset(meng, PT[64:128, c0:c0 + 64], 0.0, [BP])
                        pend.append((g, h, j, js, src, sub, c0, c1, PT, BP))
                        while len(pend) > (3 if g == 0 else LA):
                            emit_pv(pend.pop(0))
                        if dchunks and it % every_d == 0:
                            dchunks.pop(0)()
            while pend:
                emit_pv(pend.pop(0))
            while dchunks:
                dchunks.pop(0)()
            while dpend:
                dpend.pop(0)()
        P.barrier()

        with ExitStack() as st:
            gc = P.sb(st, "gcD", [128, 43], F32)
            B_gc = Buf()
            xn = [P.sb(st, f"xnD{i}", [128, 8, NT], BF16) for i in range(2)]
            B_xn = [Buf(), Buf()]
            ys = [[P.sb(st, f"yD{i}_{b}", [128, 4, NT], BF16) for b in range(2)] for i in range(3)]
            B_ys = [[Buf(), Buf()] for _ in range(3)]
            yz2 = [[P.sb(st, f"yz{i}_{b}", [128, 4, NT], BF16) for i in range(3)] for b in range(2)]
            B_yz2 = [[Buf() for _ in range(3)] for _ in range(2)]
            sz = [P.sb(st, f"sz{i}", [128, NT], F32) for i in range(2)]
            B_sz = [Buf(), Buf()]
            gsb = [P.sb(st, f"gsb{i}", [128, NT], F32) for i in range(3)]
            B_gsb = [Buf() for _ in range(3)]
            acc = [P.sb(st, f"acc{i}", [128, NT], F32) for i in range(2)]
            B_acc = [Buf(), Buf()]
            tmp = [P.sb(st, f"tmpD{i}", [128, NT], F32) for i in range(2)]
            B_tmp = [Buf(), Buf()]
            mg = [P.sb(st, f"mg{i}", [128, 8, NT], BF16) for i in range(2)]
            B_mg = [Buf(), Buf()]
            P.dma("sp", gc[:], gcols[l], w=[B_gc])
            rot = Rot([(psum[i], psB[i]) for i in range(8)])
            ysrc = [YA, YB, YC]
            si = 0
            gi = 0
            ti = 0
            for t in range(NTILE):
                t0 = t * NT
                XNt, BXN = xn[t % 2], B_xn[t % 2]
                yz, B_yz = yz2[t % 2], B_yz2[t % 2]
                P.dma("sp", XNt[:], XN[:, :, t0:t0 + NT], w=[BXN])
                for i in range(3):
                    P.dma("sp", ys[i][t % 2][:], ysrc[i][:, :, t0:t0 + NT], w=[B_ys[i][t % 2]])
                for i in range(3):
                    for c in range(4):
                        ps, pb = rot.next()
                        for kc in range(8):
                            P.mm(ps[:], wD_sb[:, kc, i * 512 + c * 128:i * 512 + (c + 1) * 128], XNt[:, kc, :], kc == 0, kc == 7,
                                 [B_wD, BXN], [pb])
                        SZ, BSZ = sz[si % 2], B_sz[si % 2]
                        si += 1
                        P.act(SZ[:], ps[:], AF.Silu, [], [pb, BSZ])
                        P.tt("dve" if si % 2 else "pool", yz[i][:, c, :], ys[i][t % 2][:, c, :], SZ[:], ALU.mult,
                             [B_ys[i][t % 2], BSZ], [B_yz[i]])
                MGt, BMG = mg[t % 2], B_mg[t % 2]
                for m in range(8):
                    AC, BAC = acc[m % 2], B_acc[m % 2]
                    for i in range(3):
                        ps, pb = rot.next()
                        off = 1536 + i * 1024 + m * 128
                        for kc in range(8):
                            P.mm(ps[:], wD_sb[:, kc, off:off + 128], XNt[:, kc, :], kc == 0, kc == 7, [B_wD, BXN], [pb])
                        G, BG = gsb[gi % 3], B_gsb[gi % 3]
                        gi += 1
                        P.act(G[:], ps[:], AF.Sigmoid, [B_gc], [pb, BG], bias=gc[:, 16 + i * 8 + m:17 + i * 8 + m])
                        ps2, pb2 = rot.next()
                        for kc in range(4):
                            P.mm(ps2[:], wbr_sb[:, i * 4 + kc, m * 128:(m + 1) * 128], yz[i][:, kc, :], kc == 0, kc == 3,
                                 [B_wD, B_yz[i]], [pb2])
                        if i == 0:
                            P.tt("dve", AC[:], ps2[:], G[:], ALU.mult, [BG], [pb2, BAC])
                        else:
                            T_, BT = tmp[ti % 2], B_tmp[ti % 2]
                            ti += 1
                            P.tt("dve", T_[:], ps2[:], G[:], ALU.mult, [BG], [pb2, BT])
                            if i == 1:
                                P.tt("pool", AC[:], AC[:], T_[:], ALU.add, [BT], [BAC])
                            else:
                                P.tt("pool", MGt[:, m, :], AC[:], T_[:], ALU.add, [BT, BAC], [BMG])
                P.dma("pool", MG[:, :, t0:t0 + NT], MGt[:], r=[BMG])
        carryD.close()
        P.barrier()

        if not last:
            carryA = ExitStack()
            wA_sb = P.sb(carryA, "wA_sb", [128, 8, W_A], BF16)
            B_wA = Buf()
        with ExitStack() as st:
            wo_sb = P.sb(st, "wo_sb", [128, 8, D], BF16)
            B_wo = Buf()
            stage = [P.sb(st, f"stageE{i}", [128, 1024], F32) for i in range(2)]
            stageB = [Buf(), Buf()]
            gc = P.sb(st, "gcE", [128, 43], F32)
            B_gc = Buf()
            mgt = [P.sb(st, f"mgE{i}", [128, 8, NT], BF16) for i in range(2)]
            B_mgt = [Buf(), Buf()]
            xt = [P.sb(st, f"xtE{i}", [128, 8, NT], F32) for i in range(2)]
            B_xt = [Buf(), Buf()]
            o_sb = P.sb(st, "o_sb", [128, 8, NT], F32)
            B_o = Buf()
            sq = P.sb(st, "sqE", [128, 8, NT], BF16)
            B_sq = Buf()
            rstd = P.sb(st, "rstdE", [128, NT], F32)
            B_rstd = Buf()
            tmp = [P.sb(st, f"tmpE{i}", [128, NT], F32) for i in range(3)]
            B_tmp = [Buf(), Buf(), Buf()]
            xo = [P.sb(st, f"xoE{i}", [128, 8, NT], F32) for i in range(2)]
            B_xo = [Buf(), Buf()]
            k = [0]
            for c in range(8):
                for ch in load_chunks(wo_sb[:, c, :], B_wo, wout[l, c * 128:(c + 1) * 128, :], D, stage, stageB, k, 1024):
                    ch()
            achunks = []
            apend = []
            if not last:
                for c in range(8):
                    achunks += load_chunks(wA_sb[:, c, :], B_wA, wA[l + 1, c * 128:(c + 1) * 128, :], W_A, stage, stageB, k, 1024, "pool", apend)
            P.dma("sp", gc[:], gcols[l], w=[B_gc])
            rot = Rot([(psum[i], psB[i]) for i in range(1, 8)])
            o_sb2 = [o_sb, P.sb(st, "o_sbb", [128, 8, NT], F32)]
            B_o2 = [B_o, Buf()]
            sq2 = [sq, P.sb(st, "sqEb", [128, 8, NT], BF16)]
            B_sq2 = [B_sq, Buf()]

            def outproj(t):
                t0 = t * NT
                M_, BM = mgt[t % 2], B_mgt[t % 2]
                X, BX = xt[t % 2], B_xt[t % 2]
                O_, BO = o_sb2[t % 2], B_o2[t % 2]
                SQ, BSQ = sq2[t % 2], B_sq2[t % 2]
                P.dma("sp", M_[:], MG[:, :, t0:t0 + NT], w=[BM])
                P.dma("sp", X[:], x_src[:, :, t0:t0 + NT], w=[BX])
                for m in range(8):
                    ps, pb = rot.next()
                    for kc in range(8):
                        P.mm(ps[:], wo_sb[:, kc, m * 128:(m + 1) * 128], M_[:, kc, :], kc == 0, kc == 7, [B_wo, BM], [pb])
                    P.copy("dve", O_[:, m, :], ps[:], [], [pb, BO])
                    P.act(SQ[:, m, :], ps[:], AF.Square, [], [pb, BSQ])

            def post(t):
                t0 = t * NT
                X, BX = xt[t % 2], B_xt[t % 2]
                XO, BXO = xo[t % 2], B_xo[t % 2]
                O_, BO = o_sb2[t % 2], B_o2[t % 2]
                SQ, BSQ = sq2[t % 2], B_sq2[t % 2]
                for m in range(8):
                    P.mm(psum[0][:], ones_bf[:], SQ[:, m, :], m == 0, m == 7, [B_ones, BSQ], [psB[0]])
                P.act(rstd[:], psum[0][:], AF.Ln, [], [psB[0], B_rstd], bias=EPS, scale=1.0 / D)
                P.act(rstd[:], rstd[:], AF.Exp, [], [B_rstd], scale=-0.5)
                for m in range(8):
                    T_, BT = tmp[m % 3], B_tmp[m % 3]
                    P.stt("dve", T_[:], O_[:, m, :], gc[:, 8 + m:9 + m], rstd[:], ALU.mult, ALU.mult, [BO, B_gc, B_rstd], [BT])
                    P.tt("pool", XO[:, m, :], X[:, m, :], T_[:], ALU.add, [BX, BT], [BXO])
                d_ = P.dma("pool", x_dst[:, :, t0:t0 + NT], XO[:], r=[BXO])
                if last:
                    final_ops.append(d_)

            outproj(0)
            per_t = (len(achunks) + NTILE - 1) // NTILE
            for t in range(NTILE):
                if t + 1 < NTILE:
                    outproj(t + 1)
                for _ in range(per_t):
                    if achunks:
                        achunks.pop(0)()
                post(t)
            while achunks:
                achunks.pop(0)()
            while apend:
                apend.pop(0)()
        P.barrier()

    P.finish(final_ops)
    return nc


def prep_weights(inp, L, S):
    f32 = np.float32
    w_in = np.asarray(inp["w_in"], f32)[:L]
    sl = lambda a, b: w_in[:, :, a:b]
    kr = sl(1920, 1952)
    krx = np.concatenate([kr, kr[:, :, 16:32], kr[:, :, 0:16]], axis=2)
    wA = np.ascontiguousarray(np.concatenate(
        [sl(0, 512), sl(512, 1024), sl(1536, 1792), sl(1792, 1920), krx, sl(2464, 2976), sl(2976, 3488), sl(3488, 4000)], axis=2))
    wD = np.ascontiguousarray(np.concatenate([sl(1024, 1536), sl(1952, 2464), sl(4000, 4512), sl(4512, 7584)], axis=2))
    assert wA.shape[2] == W_A and wD.shape[2] == W_D
    uq = np.asarray(inp["mla_w_uq"], f32)[:L].reshape(L, 256, 8, 96)
    wuq = np.ascontiguousarray(np.concatenate([uq[..., 64:96], uq[..., 80:96], uq[..., 64:80], uq[..., 0:64]], axis=3).reshape(L, 256, 1024))
    wukv = np.ascontiguousarray(np.asarray(inp["mla_w_ukv"], f32)[:L])
    wbr = np.ascontiguousarray(np.asarray(inp["w_branch"], f32)[:L])
    wout = np.ascontiguousarray(np.asarray(inp["w_out"], f32)[:L])
    gcols = np.zeros((L, 128, 43), f32)
    gcols[:, :, 0:8] = np.asarray(inp["pre_g"], f32)[:L].reshape(L, 8, 128).transpose(0, 2, 1)
    gcols[:, :, 8:16] = np.asarray(inp["post_g"], f32)[:L].reshape(L, 8, 128).transpose(0, 2, 1)
    gcols[:, :, 16:40] = np.asarray(inp["gate_b"], f32)[:L].reshape(L, 3, 8, 128).transpose(0, 3, 1, 2).reshape(L, 128, 24)
    gcols[:, :, 40:42] = np.asarray(inp["mla_q_norm_g"], f32)[:L].reshape(L, 2, 128).transpose(0, 2, 1)
    gcols[:, :, 42] = np.asarray(inp["mla_kv_norm_g"], f32)[:L]
    lnrow = np.ascontiguousarray(np.stack([np.asarray(inp["sgu_ln_g"], f32)[:L], np.asarray(inp["sgu_ln_b"], f32)[:L]], axis=1))
    sb_ = np.asarray(inp["sgu_b"], f32)[:L]
    bsb = np.ascontiguousarray(np.repeat(sb_.reshape(L, 4, 2, 1, 128), 64, axis=3).transpose(0, 2, 3, 1, 4).reshape(L, 128, 4, 128))
    wsT = np.ascontiguousarray(np.asarray(inp["sgu_w"], f32)[:L].transpose(0, 3, 1, 2))
    rel = np.asarray(inp["ca_rel_bias"], f32)[:L]
    ki = np.arange(128)[:, None]
    cc = np.arange(RELW)[None, :]
    idx = np.clip(cc - 384 - ki, -128, 128) + 128
    relM = np.ascontiguousarray(rel[:, :, idx].transpose(0, 2, 1, 3))
    tri = (np.arange(128)[:, None] <= np.arange(128)[None, :]).astype(f32)
    half = 16
    inv = (np.float32(10000.0) ** (-np.arange(half, dtype=f32) / half)).astype(f32)
    ang = (np.arange(S, dtype=f32)[:, None] * inv[None, :]).astype(f32)
    cos = np.cos(ang).astype(f32).T
    sin = np.sin(ang).astype(f32).T
    c2 = np.concatenate([cos, cos], 0)
    s2 = np.concatenate([-sin, sin], 0)
    cs4 = np.ascontiguousarray(np.concatenate([c2, s2], 0))
    return dict(wA=wA, wD=wD, wuq=wuq, wukv=wukv, wbr=wbr, wout=wout, gcols=gcols, lnrow=lnrow, bsb=bsb, wsT=wsT,
                relM=relM, cs4=cs4, tri=tri)


_CORE_OF_BATCH = [0, 1, 4, 5]


def kernel(**inputs):
    x = np.asarray(inputs["x"], np.float32)
    B, S, _ = x.shape
    L = DEPTH
    w = prep_weights(inputs, L, S)
    nc = build_program(S, L)
    in_maps = []
    core_batch = {c: b for b, c in enumerate(_CORE_OF_BATCH)}
    zeros_x = np.zeros((D, S), np.float32)
    for core in range(8):
        m = dict(w)
        m["xT"] = np.ascontiguousarray(x[core_batch[core]].T) if core in core_batch else zeros_x
        in_maps.append(m)
    res = run_bass_kernel_spmd(nc, in_maps, core_ids=list(range(8)))
    out = np.empty((B, S, D), np.float32)
    for b, c in enumerate(_CORE_OF_BATCH):
        out[b] = res.results[c]["yout"].T
    return out
```
